# Optimizing a Trainium2 kernel written in Bass

```python
import math
import jax, jax.numpy as jnp
from jax import lax
import numpy as np

D_MODEL = 1024
BATCH = 8
SEQ = 2048
DEPTH = 4

MEM_TOKENS = 256
GRID_W = 64
A_HEADS = 4
A_HEAD_DIM = D_MODEL // 8
A_WIDTH = A_HEADS * A_HEAD_DIM
A_CONV = 5
A_CHUNK = 128
B_HEADS = 8
B_HEAD_DIM = D_MODEL // 16
B_WIDTH = B_HEADS * B_HEAD_DIM
NA_ROWS = 8
NA_COLS = 16
C_HEADS = 8
C_HEAD_DIM = D_MODEL // 16
C_WIDTH = C_HEADS * 2 * C_HEAD_DIM
Q_BLOCK = 128
X_HEADS = 4
X_HEAD_DIM = D_MODEL // X_HEADS
FFN_HIDDEN = -(-8 * D_MODEL // (3 * 256)) * 256
ROPE_THETA = 10000.0
LN_EPS = 1e-5
DEEPNORM_ALPHA = (2 * DEPTH) ** 0.25
DEEPNORM_BETA = (8 * DEPTH) ** -0.25
N_EVEN = (DEPTH + 1) // 2
N_ODD = DEPTH // 2
EVEN_IN = 4 * A_WIDTH + 4 * A_HEADS + 3 * B_WIDTH
STATE_NEG = -1e30

kernel_name = 'hybrid_mlstm_natten_diffattn_encoder'


def layer_norm(x, g, b):
    xf = x.astype(jnp.float32)
    mu = jnp.mean(xf, axis=-1, keepdims=True)
    var = jnp.mean(jnp.square(xf - mu), axis=-1, keepdims=True)
    return ((xf - mu) * lax.rsqrt(var + LN_EPS)).astype(x.dtype) * g + b


def rms_norm(x, g):
    xf = x.astype(jnp.float32)
    y = xf * lax.rsqrt(jnp.mean(jnp.square(xf), axis=-1, keepdims=True) + LN_EPS)
    return y.astype(x.dtype) * g


def split_cols(t, widths):
    offs = np.cumsum(widths)[:-1].tolist()
    return jnp.split(t, offs, axis=-1)


def centred_depthwise_conv(x, w):
    K, C = w.shape
    return lax.conv_general_dilated(
        x, w.reshape(K, 1, C).astype(x.dtype), window_strides=(1,),
        padding=[(K // 2, K // 2)], dimension_numbers=('NWC', 'WIO', 'NWC'),
        feature_group_count=C)


def rope(t):
    S, d = t.shape[1], t.shape[-1]
    inv_freq = ROPE_THETA ** (-jnp.arange(0, d, 2, dtype=jnp.float32) / d)
    ang = jnp.arange(S, dtype=jnp.float32)[:, None] * inv_freq[None, :]
    cos = jnp.cos(ang)[:, None, :]
    sin = jnp.sin(ang)[:, None, :]
    tf = t.astype(jnp.float32)
    t1, t2 = tf[..., : d // 2], tf[..., d // 2:]
    return jnp.concatenate([t1 * cos - t2 * sin, t1 * sin + t2 * cos], axis=-1).astype(t.dtype)


def mlstm_chunkwise(q, k, v, log_i, log_f):
    N, H, S, d = q.shape
    L = min(A_CHUNK, S)
    NC = S // L
    qc = q.reshape(N, H, NC, L, d)
    kc = k.reshape(N, H, NC, L, d)
    vc = v.reshape(N, H, NC, L, d)
    ai = log_i.reshape(N, H, NC, L)
    b = jnp.cumsum(log_f.reshape(N, H, NC, L), axis=-1)
    g = b[..., -1]
    w_end = g[..., None] - b + ai
    m_loc = jnp.max(w_end, axis=-1)
    p = jnp.exp(w_end - m_loc[..., None])
    kv_loc = jnp.einsum('nhcl,nhcld,nhcle->nhcde', p, kc, vc)
    k_loc = jnp.einsum('nhcl,nhcld->nhcd', p, kc)

    def step(carry, inp):
        C, nv, m = carry
        g_c, m_c, kv_c, k_c = inp
        m_new = jnp.maximum(g_c + m, m_c)
        s_old = jnp.exp(g_c + m - m_new)
        s_new = jnp.exp(m_c - m_new)
        C_new = s_old[..., None, None] * C + s_new[..., None, None] * kv_c
        n_new = s_old[..., None] * nv + s_new[..., None] * k_c
        return (C_new, n_new, m_new), (C, nv, m)

    init = (jnp.zeros((N, H, d, d), jnp.float32), jnp.zeros((N, H, d), jnp.float32),
            jnp.full((N, H), STATE_NEG, jnp.float32))
    xs = (jnp.moveaxis(g, 2, 0), jnp.moveaxis(m_loc, 2, 0),
          jnp.moveaxis(kv_loc, 2, 0), jnp.moveaxis(k_loc, 2, 0))
    _, (C_in, n_in, m_in) = lax.scan(step, init, xs)
    C_in = jnp.moveaxis(C_in, 0, 2)
    n_in = jnp.moveaxis(n_in, 0, 2)
    m_in = jnp.moveaxis(m_in, 0, 2)
    upto = jnp.tril(jnp.ones((L, L), dtype=bool))
    D = jnp.where(upto, b[..., :, None] - b[..., None, :] + ai[..., None, :], -jnp.inf)
    m_inter = b + m_in[..., None]
    m = jnp.maximum(jnp.max(D, axis=-1), m_inter)
    Smat = jnp.einsum('nhcld,nhcsd->nhcls', qc, kc) * jnp.exp(D - m[..., None])
    s_inter = jnp.exp(m_inter - m)
    num = (jnp.einsum('nhcls,nhcse->nhcle', Smat, vc)
           + s_inter[..., None] * jnp.einsum('nhcld,nhcde->nhcle', qc, C_in))
    den = jnp.sum(Smat, axis=-1) + s_inter * jnp.einsum('nhcld,nhcd->nhcl', qc, n_in)
    h = num / jnp.maximum(jnp.abs(den), jnp.exp(-m))[..., None]
    return h.reshape(N, H, S, d)


def neighbourhood_attention(q, k, v, rpb):
    B_, S, H, d = q.shape
    R = S // GRID_W
    kr = min(NA_ROWS, R)
    q = q.reshape(B_, R, GRID_W, H, d) * (d ** -0.5)
    k = k.reshape(B_, R, GRID_W, H, d)
    v = v.reshape(B_, R, GRID_W, H, d)
    rows = jnp.arange(R)
    row_start = jnp.clip(rows - NA_ROWS // 2, 0, R - kr)
    key_rows = row_start[:, None] + jnp.arange(kr)[None, :]
    k_blk = k[:, key_rows]
    v_blk = v[:, key_rows]
    cols = jnp.arange(GRID_W)
    col_start = jnp.clip(cols - NA_COLS // 2, 0, GRID_W - NA_COLS)
    in_win = (cols[None, :] >= col_start[:, None]) & (cols[None, :] < col_start[:, None] + NA_COLS)
    rel_r = key_rows - rows[:, None] + (NA_ROWS - 1)
    rel_c = jnp.clip(cols[None, :] - cols[:, None] + (NA_COLS - 1), 0, 2 * NA_COLS - 2)
    bias = rpb[:, rel_r[:, None, :, None], rel_c[None, :, None, :]]
    s = jnp.einsum('brqhd,brikhd->bhrqik', q, k_blk).astype(jnp.float32) + bias.astype(jnp.float32)
    s = jnp.where(in_win[:, None, :], s, -jnp.inf)
    p = jax.nn.softmax(s.reshape(B_, H, R, GRID_W, kr * GRID_W), axis=-1)
    p = p.reshape(B_, H, R, GRID_W, kr, GRID_W).astype(v.dtype)
    out = jnp.einsum('bhrqik,brikhd->brqhd', p, v_blk)
    return out.reshape(B_, S, H * d)


def diff_attention(q, k, v, lam):
    B_, S, H, _, d = q.shape
    nb = S // Q_BLOCK
    qb = jnp.moveaxis(q.reshape(B_, nb, Q_BLOCK, H, 2, d), 1, 0)
    scale = d ** -0.5

    def block(qi):
        s = jnp.einsum('bqhcd,bkhcd->bhcqk', qi, k).astype(jnp.float32) * scale
        p = jax.nn.softmax(s, axis=-1)
        a = p[:, :, 0] - lam * p[:, :, 1]
        return jnp.einsum('bhqk,bkhe->bqhe', a.astype(v.dtype), v)

    out = lax.map(block, qb)
    return jnp.moveaxis(out, 0, 1).reshape(B_, S, H, 2 * d)


def memory_cross_attention(x, mem, wq, wkv, wo):
    B_, S, _ = x.shape
    M = mem.shape[1]
    q = (x @ wq).reshape(B_, S, X_HEADS, X_HEAD_DIM) * (X_HEAD_DIM ** -0.5)
    k, v = jnp.split(mem @ wkv, 2, axis=-1)
    k = k.reshape(B_, M, X_HEADS, X_HEAD_DIM)
    v = v.reshape(B_, M, X_HEADS, X_HEAD_DIM)
    p = jax.nn.softmax(jnp.einsum('bshd,bmhd->bhsm', q, k).astype(jnp.float32), axis=-1)
    o = jnp.einsum('bhsm,bmhd->bshd', p.astype(v.dtype), v)
    return o.reshape(B_, S, D_MODEL) @ wo


def swiglu(x, w_gu, w_down):
    gate, up = jnp.split(x @ w_gu, 2, axis=-1)
    return (jax.nn.silu(gate) * up) @ w_down


def even_mixer(x, w_in, gate_b, conv_w, rpb, w_out):
    B_, S, _ = x.shape
    a_qk, a_v, a_o, a_g, b_q, b_k, b_v = split_cols(
        x @ w_in, [2 * A_WIDTH, A_WIDTH, A_WIDTH, 4 * A_HEADS, B_WIDTH, B_WIDTH, B_WIDTH])
    a_qk = jax.nn.silu(centred_depthwise_conv(a_qk, conv_w))
    a_q, a_k = jnp.split(a_qk, 2, axis=-1)

    def heads(t):
        return t.reshape(B_, S, A_HEADS, A_HEAD_DIM).transpose(0, 2, 1, 3).astype(jnp.float32)

    q, k, v = heads(a_q), heads(a_k) * (A_HEAD_DIM ** -0.5), heads(a_v)
    gates = (a_g.astype(jnp.float32) + gate_b.astype(jnp.float32)).reshape(B_, S, 4, A_HEADS).transpose(2, 0, 3, 1)

    def both_dirs(t):
        return jnp.concatenate([t, jnp.flip(t, axis=2)], axis=0)

    log_i = jnp.concatenate([gates[0], jnp.flip(gates[1], axis=-1)], axis=0)
    log_f = jax.nn.log_sigmoid(jnp.concatenate([gates[2], jnp.flip(gates[3], axis=-1)], axis=0))
    h2 = mlstm_chunkwise(both_dirs(q), both_dirs(k), both_dirs(v), log_i, log_f)
    h = h2[:B_] + jnp.flip(h2[B_:], axis=2)
    h = h.transpose(0, 2, 1, 3).reshape(B_, S, A_WIDTH).astype(x.dtype)
    a_out = jax.nn.sigmoid(a_o) * h
    b_out = neighbourhood_attention(b_q.reshape(B_, S, B_HEADS, B_HEAD_DIM),
                                    b_k.reshape(B_, S, B_HEADS, B_HEAD_DIM),
                                    b_v.reshape(B_, S, B_HEADS, B_HEAD_DIM), rpb)
    return jnp.concatenate([a_out, b_out], axis=-1) @ w_out


def odd_mixer(x, w_in, lam_p, subln_g, w_out, lam_init):
    B_, S, _ = x.shape
    q, k, v = jnp.split(x @ w_in, 3, axis=-1)
    q = rope(q.reshape(B_, S, 2 * C_HEADS, C_HEAD_DIM)).reshape(B_, S, C_HEADS, 2, C_HEAD_DIM)
    k = rope(k.reshape(B_, S, 2 * C_HEADS, C_HEAD_DIM)).reshape(B_, S, C_HEADS, 2, C_HEAD_DIM)
    v = v.reshape(B_, S, C_HEADS, 2 * C_HEAD_DIM)
    lp = lam_p.astype(jnp.float32)
    lam = jnp.exp(jnp.sum(lp[0] * lp[1])) - jnp.exp(jnp.sum(lp[2] * lp[3])) + lam_init
    o = diff_attention(q, k, v, lam)
    o = rms_norm(o, subln_g) * (1.0 - lam_init)
    return o.reshape(B_, S, C_WIDTH) @ w_out


def setup_inputs(seed: int = 0) -> dict:
    key = jax.random.key(seed)
    ks = jax.random.split(key, 20)

    def nrm(kk, shape, scale):
        return jax.random.normal(kk, shape, jnp.float32) * scale

    x = nrm(ks[0], (BATCH, SEQ, D_MODEL), 1.0)
    mem = nrm(ks[1], (BATCH, MEM_TOKENS, D_MODEL), 1.0)
    even_w_in = nrm(ks[2], (N_EVEN, D_MODEL, EVEN_IN), D_MODEL ** -0.5)
    i_bias = nrm(ks[3], (N_EVEN, 2 * A_HEADS), 0.1)
    f_lin = jnp.linspace(3.0, 6.0, A_HEADS, dtype=jnp.float32)
    f_bias = jnp.concatenate([f_lin, f_lin])[None, :] + nrm(ks[4], (N_EVEN, 2 * A_HEADS), 0.1)
    even_gate_b = jnp.concatenate([i_bias, f_bias], axis=-1)
    even_conv_w = nrm(ks[5], (N_EVEN, A_CONV, 2 * A_WIDTH), A_CONV ** -0.5)
    even_rpb = nrm(ks[6], (N_EVEN, B_HEADS, 2 * NA_ROWS - 1, 2 * NA_COLS - 1), 0.02)
    even_w_out = nrm(ks[7], (N_EVEN, A_WIDTH + B_WIDTH, D_MODEL), (A_WIDTH + B_WIDTH) ** -0.5 * DEEPNORM_BETA)
    odd_w_in = nrm(ks[8], (N_ODD, D_MODEL, 3 * C_WIDTH), D_MODEL ** -0.5)
    odd_lambda = nrm(ks[9], (N_ODD, 4, C_HEAD_DIM), 0.1)
    odd_subln_g = 1.0 + nrm(ks[10], (N_ODD, 2 * C_HEAD_DIM), 0.02)
    odd_w_out = nrm(ks[11], (N_ODD, C_WIDTH, D_MODEL), C_WIDTH ** -0.5 * DEEPNORM_BETA)
    mem_wq = nrm(ks[12], (DEPTH, D_MODEL, D_MODEL), D_MODEL ** -0.5)
    mem_wkv = nrm(ks[13], (DEPTH, D_MODEL, 2 * D_MODEL), D_MODEL ** -0.5)
    mem_wo = nrm(ks[14], (DEPTH, D_MODEL, D_MODEL), D_MODEL ** -0.5 * DEEPNORM_BETA)
    ffn_w_gu = nrm(ks[15], (DEPTH, D_MODEL, 2 * FFN_HIDDEN), D_MODEL ** -0.5)
    ffn_w_down = nrm(ks[16], (DEPTH, FFN_HIDDEN, D_MODEL), FFN_HIDDEN ** -0.5 * DEEPNORM_BETA)
    ln_g = 1.0 + nrm(ks[17], (DEPTH, 3, D_MODEL), 0.02)
    ln_b = nrm(ks[18], (DEPTH, 3, D_MODEL), 0.02)
    return {'x': x, 'mem': mem, 'even_w_in': even_w_in, 'even_gate_b': even_gate_b,
            'even_conv_w': even_conv_w, 'even_rpb': even_rpb, 'even_w_out': even_w_out,
            'odd_w_in': odd_w_in, 'odd_lambda': odd_lambda, 'odd_subln_g': odd_subln_g,
            'odd_w_out': odd_w_out, 'mem_wq': mem_wq, 'mem_wkv': mem_wkv, 'mem_wo': mem_wo,
            'ffn_w_gu': ffn_w_gu, 'ffn_w_down': ffn_w_down, 'ln_g': ln_g, 'ln_b': ln_b}


def reference(x, mem, even_w_in, even_gate_b, even_conv_w, even_rpb, even_w_out,
              odd_w_in, odd_lambda, odd_subln_g, odd_w_out, mem_wq, mem_wkv, mem_wo,
              ffn_w_gu, ffn_w_down, ln_g, ln_b):
    for layer in range(DEPTH):
        j = layer // 2
        if layer % 2 == 0:
            mix = even_mixer(x, even_w_in[j], even_gate_b[j], even_conv_w[j], even_rpb[j], even_w_out[j])
        else:
            lam_init = 0.8 - 0.6 * math.exp(-0.3 * layer)
            mix = odd_mixer(x, odd_w_in[j], odd_lambda[j], odd_subln_g[j], odd_w_out[j], lam_init)
        x = layer_norm(DEEPNORM_ALPHA * x + mix, ln_g[layer, 0], ln_b[layer, 0])
        xa = memory_cross_attention(x, mem, mem_wq[layer], mem_wkv[layer], mem_wo[layer])
        x = layer_norm(DEEPNORM_ALPHA * x + xa, ln_g[layer, 1], ln_b[layer, 1])
        ff = swiglu(x, ffn_w_gu[layer], ffn_w_down[layer])
        x = layer_norm(DEEPNORM_ALPHA * x + ff, ln_g[layer, 2], ln_b[layer, 2])
    return x
```

```python
import math
import numpy as np
import concourse.bass as bass
import concourse.mybir as mybir
from concourse.bass_utils import run_bass_kernel_spmd

F32 = mybir.dt.float32
BF16 = mybir.dt.bfloat16
AF = mybir.ActivationFunctionType
ALU = mybir.AluOpType
AX = mybir.AxisListType

D = 1024
S = 2048
NT = 16
DEPTH = 4
MEM = 256
FFN = 2816
ALPHA = (2 * DEPTH) ** 0.25
EPS = 1e-5
EVEN_IN = 3600
NEG = -30000.0


class Prog:
    NSLOT = 8

    def __init__(self, nc):
        self.nc = nc
        self.ops = []
        self.lastw = {}
        self.readers = {}
        self.pending_barrier = None
        self.final_wait = []
        self.eng_obj = {'pe': nc.tensor, 'act': nc.scalar, 'dve': nc.vector,
                        'pool': nc.gpsimd, 'sp': nc.sync}

    def add(self, eng, fn, reads=(), writes=(), dma=False):
        i = len(self.ops)
        deps = set()
        for r in reads:
            if r in self.lastw:
                deps.add(self.lastw[r])
        for w in writes:
            if w in self.lastw:
                deps.add(self.lastw[w])
            deps.update(self.readers.get(w, {}).values())
        if self.pending_barrier and eng in self.pending_barrier:
            deps.update(self.pending_barrier.pop(eng))
        if eng == 'pe' and not dma:
            deps = set(p for p in deps if not (self.ops[p][0] == 'pe' and not self.ops[p][3]))
        for r in reads:
            rd = self.readers.setdefault(r, {})
            if dma:
                rd[(eng, i)] = i
            else:
                rd[eng] = i
        for w in writes:
            self.lastw[w] = i
            self.readers[w] = {}
        deps.discard(i)
        self.ops.append([eng, fn, deps, dma])
        return i

    def barrier(self):
        last = {}
        dmas = {}
        for i, op in enumerate(self.ops):
            if op[3]:
                dmas.setdefault(op[0], []).append(i)
            else:
                last[op[0]] = i
        deps = set(last.values())
        for q, lst in dmas.items():
            deps.update(lst[-self.NSLOT:])
        self.pending_barrier = {e: set(deps) for e in self.eng_obj}
        self.lastw = {}
        self.readers = {}

    def emit(self):
        nc = self.nc
        ops = self.ops
        flagged = set()
        for op in ops:
            for p in op[2]:
                if not ops[p][3]:
                    flagged.add(p)
        engsem = {e: nc.alloc_semaphore("sem_" + e) for e in self.eng_obj}
        dmasem = {e: [nc.alloc_semaphore("dsem_%s_%d" % (e, k)) for k in range(self.NSLOT)]
                  for e in ('sp', 'pool')}
        cnt = {e: 0 for e in self.eng_obj}
        dcnt = {e: 0 for e in self.eng_obj}
        semval = {}
        waited = {e: {} for e in self.eng_obj}
        nwaits = 0
        for i, (eng, fn, deps, dma) in enumerate(ops):
            E = self.eng_obj[eng]
            need = {}
            for p in deps:
                sem, v = semval[p]
                key = id(sem)
                if key not in need or need[key][1] < v:
                    need[key] = (sem, v)
            if dma:
                n = dcnt[eng]
                slot = dmasem[eng][n % self.NSLOT]
                prev = 16 * (n // self.NSLOT)
                if prev > 0:
                    key = id(slot)
                    if key not in need or need[key][1] < prev:
                        need[key] = (slot, prev)
            for key, (sem, v) in need.items():
                if waited[eng].get(key, 0) >= v:
                    continue
                E.wait_ge(sem, v)
                nwaits += 1
                waited[eng][key] = v
            inst = fn()
            if dma:
                inst.then_inc(slot, 16)
                semval[i] = (slot, 16 * (n // self.NSLOT + 1))
                dcnt[eng] += 1
            elif i in flagged:
                cnt[eng] += 1
                inst.then_inc(engsem[eng], 1)
                semval[i] = (engsem[eng], cnt[eng])
        for i in self.final_wait:
            sem, v = semval[i]
            nc.sync.wait_ge(sem, v)
        return dict(n_ops=len(ops), n_waits=nwaits, cnt=cnt, dcnt=dcnt)


def na_tables():
    R, W, NR, NCW = 32, 64, 8, 16
    rows = np.arange(R)
    row_start = np.clip(rows - NR // 2, 0, R - NR)
    cols = np.arange(W)
    col_start = np.clip(cols - NCW // 2, 0, W - NCW)
    tok_r = np.arange(S) // W
    tok_c = np.arange(S) % W
    variants = []
    vkey = {}
    plan = []
    for jt in range(NT):
        qt = np.arange(jt * 128, (jt + 1) * 128)
        qr, qc = tok_r[qt], tok_c[qt]
        lst = []
        for kt in range(NT):
            k = np.arange(kt * 128, (kt + 1) * 128)
            kr, kc = tok_r[k], tok_c[k]
            okr = (kr[:, None] >= row_start[qr][None, :]) & (kr[:, None] < row_start[qr][None, :] + NR)
            okc = (kc[:, None] >= col_start[qc][None, :]) & (kc[:, None] < col_start[qc][None, :] + NCW)
            ok = okr & okc
            if not ok.any():
                continue
            rel_r = kr[:, None] - qr[None, :] + (NR - 1)
            rel_c = np.clip(kc[:, None] - qc[None, :] + (NCW - 1), 0, 2 * NCW - 2)
            idx = np.where(ok, rel_r * (2 * NCW - 1) + rel_c, -1).astype(np.int32)
            key = idx.tobytes()
            if key not in vkey:
                vkey[key] = len(variants)
                variants.append(idx)
            lst.append((kt, vkey[key]))
        plan.append(lst)
    return variants, plan


_NA_VARIANTS, _NA_PLAN = na_tables()
NVAR = len(_NA_VARIANTS)


def host_prep(inp):
    f = np.float32
    c = {}
    c['ident'] = np.eye(128, dtype=f)
    c['trif'] = np.triu(np.ones((128, 128), f))
    c['trib'] = np.tril(np.ones((128, 128), f))
    c['ones'] = np.ones((128, 128), f)
    d = 64
    inv_freq = 10000.0 ** (-np.arange(0, d, 2, dtype=np.float64) / d)
    ang = np.arange(S, dtype=np.float64)[None, :] * inv_freq[:, None]
    p = np.arange(128)
    cosT = np.cos(ang)[p % 32, :]
    sign = np.where((p % 64) < 32, -1.0, 1.0)[:, None]
    sinT = np.sin(ang)[p % 32, :] * sign
    c['cosT'] = cosT.astype(f)
    c['sinT'] = sinT.astype(f)
    i = np.arange(2048)
    partner = (i // 64) * 64 + ((i % 64) + 32) % 64
    pm = np.zeros((128, 128), f)
    cc_ = np.arange(128)
    pm[(cc_ // 64) * 64 + ((cc_ % 64) + 32) % 64, cc_] = 1.0
    c['permm'] = pm
    cw = inp['even_conv_w']
    c['conv_w'] = np.ascontiguousarray(cw.reshape(2, 5, 8, 128).transpose(0, 3, 2, 1))
    gb = inp['even_gate_b']
    c['gate_b'] = np.ascontiguousarray(np.broadcast_to(np.tile(gb, (1, NT))[:, None, :], (2, 128, NT * 16)))
    rpb = inp['even_rpb']
    bm = np.full((2, 8, NVAR, 128, 128), NEG, f)
    for v, idx in enumerate(_NA_VARIANTS):
        ok = idx >= 0
        g = rpb.reshape(2, 8, -1)[:, :, np.where(ok, idx, 0)]
        bm[:, :, v] = np.where(ok[None, None], g, f(NEG))
    bm = bm.reshape(2, 4, 2, NVAR, 128, 128).transpose(0, 1, 4, 3, 2, 5)
    c['na_bm'] = np.ascontiguousarray(bm)
    c['ln_g'] = np.ascontiguousarray(np.broadcast_to(inp['ln_g'][:, :, None, :], (4, 3, 128, D)))
    c['ln_b'] = np.ascontiguousarray(np.broadcast_to(inp['ln_b'][:, :, None, :], (4, 3, 128, D)))
    c['subln_g'] = np.ascontiguousarray(np.broadcast_to(np.tile(inp['odd_subln_g'], (1, 8))[:, None, :], (2, 128, D)))
    c['subln_c'] = np.ascontiguousarray(inp['odd_subln_g'].reshape(2, 128, 1))
    c['lam_p'] = np.ascontiguousarray(np.broadcast_to(inp['odd_lambda'].reshape(2, 1, 256), (2, 128, 256)))
    return c


def build(nsub=3 * DEPTH):
    nc = bass.Bass("TRN2", target_bir_lowering=False)
    P = Prog(nc)

    declared = {}

    def din(name, shape):
        if name not in declared:
            declared[name] = nc.dram_tensor(name, list(shape), F32, kind="ExternalInput").ap()
        return declared[name]

    class LW:
        def __init__(self, name, shape):
            self.name, self.shape = name, shape

        def __getitem__(self, l):
            if isinstance(l, tuple):
                return din("%s_%s" % (self.name, "_".join(str(i) for i in l)), self.shape)
            return din("%s_%d" % (self.name, l), self.shape)

    x_d = din("x", [S, D])
    mem_d = din("mem", [MEM, D])
    even_w_in = LW("even_w_in", [D, EVEN_IN])
    even_w_out = LW("even_w_out", [D, D])
    odd_w_in = LW("odd_w_in", [D, 3 * D])
    odd_w_perm = LW("odd_w_perm", [D, 2 * D])
    odd_w_out = LW("odd_w_out", [D, D])
    mem_wq = LW("mem_wq", [D, D])
    mem_wkv = LW("mem_wkv", [D, 2 * D])
    mem_wo = LW("mem_wo", [D, D])
    ffn_w_gu = LW("ffn_w_gu", [D, 2 * FFN])
    ffn_w_down = LW("ffn_w_down", [FFN, D])
    ident_d = din("ident", [128, 128])
    trif_d = din("trif", [128, 128])
    trib_d = din("trib", [128, 128])
    ones_d = din("ones", [128, 128])
    permm_d = din("permm", [128, 128])
    cosT_d = din("cosT", [128, S])
    sinT_d = din("sinT", [128, S])
    conv_d = LW("conv_w", [128, 8, 5])
    gateb_d = LW("gate_b", [128, NT * 16])
    nabm_d = LW("na_bm", [128, NVAR, 2, 128])
    lng_d = LW("ln_g", [128, D])
    lnb_d = LW("ln_b", [128, D])
    subg_d = LW("subln_g", [128, D])
    subc_d = LW("subln_c", [128, 1])
    lamp_d = LW("lam_p", [128, 256])
    y_d = nc.dram_tensor("y", [S, D], F32, kind="ExternalOutput").ap()

    lo, hi = nc.bump_sbuf(212800)
    cur = [lo]

    def sb(name, shape, dt):
        nbytes = int(np.prod(shape[1:])) * (4 if dt == F32 else 2)
        nbytes = (nbytes + 31) // 32 * 32
        assert cur[0] + nbytes <= hi, (name, cur[0] + nbytes - hi)
        t = nc.alloc_sbuf_tensor_at(name + "_%d" % len(P.ops), list(shape), dt, offset=cur[0])
        cur[0] += nbytes
        return t

    ps = [nc.alloc_psum_tensor("psb%d" % i, [128, 512], F32) for i in range(8)]
    rr = [0]

    held = set()

    def bank(pool=(0, 1, 2, 3, 4, 5, 6, 7), hold=False):
        for _ in range(len(pool)):
            b = pool[rr[0] % len(pool)]
            rr[0] += 1
            if b not in held:
                if hold:
                    held.add(b)
                return b
        raise AssertionError("no free PSUM bank in pool %r (held %r)" % (pool, held))

    def free(*bs):
        for b in bs:
            held.discard(b)

    def MM(out, lhsT, rhs, start, stop, r, w, skip=False):
        P.add('pe', lambda: nc.tensor.matmul(out, lhsT=lhsT, rhs=rhs, start=start, stop=stop,
                                             skip_group_check=skip), r, w)

    def TR(out, in_, r, w):
        P.add('pe', lambda: nc.tensor.transpose(out=out, in_=in_, identity=ID[:]), list(r) + ['const'], w)

    def ACT(out, in_, func, r, w, bias=None, scale=None):
        kw = {}
        if bias is not None:
            kw['bias'] = bias
        if scale is not None:
            kw['scale'] = scale
        P.add('act', lambda: nc.scalar.activation(out=out, in_=in_, func=func, **kw), r, w)

    def TS(out, in0, s1, s2, op0, op1, r, w, eng='dve'):
        e = nc.vector if eng == 'dve' else nc.gpsimd
        if op1 is None:
            P.add(eng, lambda: e.tensor_scalar(out=out, in0=in0, scalar1=s1, scalar2=None, op0=op0), r, w)
        else:
            P.add(eng, lambda: e.tensor_scalar(out=out, in0=in0, scalar1=s1, scalar2=s2, op0=op0, op1=op1), r, w)

    def TT(out, in0, in1, op, r, w, eng='dve'):
        e = nc.vector if eng == 'dve' else nc.gpsimd
        P.add(eng, lambda: e.tensor_tensor(out=out, in0=in0, in1=in1, op=op), r, w)

    def STT(out, in0, scalar, in1, op0, op1, r, w):
        P.add('dve', lambda: nc.vector.scalar_tensor_tensor(out=out, in0=in0, scalar=scalar, in1=in1,
                                                            op0=op0, op1=op1), r, w)

    def RECIP(out, in_, r, w):
        P.add('dve', lambda: nc.vector.reciprocal(out=out, in_=in_), r, w)

    def MEMSET(ap, val, w, eng='dve'):
        e = nc.vector if eng == 'dve' else nc.gpsimd
        P.add(eng, lambda: e.memset(ap, val), (), w)

    def LOAD(out, in_, w, r=()):
        return P.add('sp', lambda: nc.sync.dma_start(out=out, in_=in_), r, w, dma=True)

    def LOADC(out, in_, w, r=()):
        return P.add('pool', lambda: nc.gpsimd.dma_start(out=out, in_=in_), r, w, dma=True)

    def wslab(src2d):
        return src2d.rearrange("(k p) n -> p k n", p=128)

    X = sb("X", [128, NT, D], F32)
    XT = sb("XT", [128, 8, S], BF16)
    MEMT = sb("MEMT", [128, 8, MEM], BF16)
    ID = sb("ID", [128, 128], F32)
    IDB = sb("IDB", [128, 128], BF16)
    TRIF = sb("TRIF", [128, 128], F32)
    TRIB = sb("TRIB", [128, 128], F32)
    ONES = sb("ONES", [128, 128], F32)
    ONESB = sb("ONESB", [128, 128], BF16)
    EPSC = sb("EPSC", [128, 4], F32)
    phase_base = cur[0]

    def XTk(j0, j1):
        return [('XT', j) for j in range(j0, j1)]

    def pipeline(items, stages, gap=1):
        items = list(items)
        n, ns = len(items), len(stages)
        for step in range(n + (ns - 1) * gap):
            for s_ in range(ns - 1, -1, -1):
                i = step - s_ * gap
                if 0 <= i < n:
                    stages[s_](items[i])

    class Pipe:
        def __init__(self, items, stages):
            self.items, self.stages = list(items), stages
            self.k = 0
            self.nsteps = len(self.items) + len(stages) - 1

        def tick(self):
            if self.k >= self.nsteps:
                return False
            for s_ in range(len(self.stages) - 1, -1, -1):
                i = self.k - s_
                if 0 <= i < len(self.items):
                    self.stages[s_](self.items[i])
            self.k += 1
            return True

        def drain(self):
            while self.tick():
                pass

    LOAD(ID[:], ident_d, ['const'])
    LOAD(TRIF[:], trif_d, ['const'])
    LOAD(TRIB[:], trib_d, ['const'])
    LOAD(ONES[:], ones_d, ['const'])
    LOADC(IDB[:], ident_d, ['constb'])
    LOADC(ONESB[:], ones_d, ['constb'])
    xv = x_d.rearrange("(j p) d -> p j d", p=128)
    for q in range(4):
        LOAD(X[:, q * 4:(q + 1) * 4, :], xv[:, q * 4:(q + 1) * 4, :], [('X', j) for j in range(q * 4, q * 4 + 4)])

    def make_XT(j, src=None, skey=None, extra=(), evac_act=False):
        if src is None:
            src, skey = X[:, j, :], ('X', j)
        for hb in range(2):
            b = bank()
            for c in range(4):
                cc = hb * 4 + c
                TR(ps[b][:, c * 128:(c + 1) * 128], src[:, cc * 128:(cc + 1) * 128], [skey] + list(extra), [('ps', b)])
            dst = XT[:, hb * 4:(hb + 1) * 4, j * 128:(j + 1) * 128]
            srcp = ps[b][:].rearrange("p (c t) -> p c t", c=4)
            if hb == 0 or evac_act:
                P.add('act', lambda dst=dst, srcp=srcp: nc.scalar.copy(out=dst, in_=srcp), (), [('ps', b), ('XT', j)])
            else:
                P.add('dve', lambda dst=dst, srcp=srcp: nc.vector.tensor_copy(out=dst, in_=srcp), (), [('ps', b), ('XT', j)])

    def ln_bufs(l, i):
        lb = dict(STAT=sb("STAT", [128, 8, 2, 6], F32), MV=sb("MV", [128, 8, 4], F32),
                  LNG=sb("LNG", [128, D], F32), LNB=sb("LNB", [128, D], F32))
        LOAD(lb['LNG'][:], lng_d[l, i], ['LN'])
        LOAD(lb['LNB'][:], lnb_d[l, i], ['LN'])
        return lb

    def ln_stages(lb):
        STAT, MV, LNG, LNB = lb['STAT'], lb['MV'], lb['LNG'], lb['LNB']

        def La(j):
            s = j % 8
            kx = ('X', j)
            st = ('STAT', s)
            for hh in range(2):
                a = STAT[:, s, hh, :]
                src = X[:, j, hh * 512:(hh + 1) * 512]
                P.add('dve', lambda a=a, src=src: nc.vector.bn_stats(out=a, in_=src), [kx], [st])
            mv = MV[:, s, 0:2]
            stv = STAT[:, s, :, :].rearrange("p a b -> p (a b)")
            P.add('dve', lambda: nc.vector.bn_aggr(out=mv, in_=stv), [st], [('MV', s)])

        def Lb(j):
            s = j % 8
            ACT(MV[:, s, 2:3], MV[:, s, 1:2], AF.Ln, [('MV', s)], [('MV2', s)], bias=EPSC[:, 0:1], scale=1.0)
            ACT(MV[:, s, 2:3], MV[:, s, 2:3], AF.Exp, [], [('MV2', s)], scale=-0.5)

        def Lc(j):
            s = j % 8
            TS(MV[:, s, 3:4], MV[:, s, 0:1], MV[:, s, 2:3], -1.0, ALU.mult, ALU.mult, [('MV', s), ('MV2', s)], [('MV3', s)])

        def Ld(j):
            s = j % 8
            kx = ('X', j)
            ACT(X[:, j, :], X[:, j, :], AF.Identity, [('MV2', s), ('MV3', s)], [kx], bias=MV[:, s, 3:4], scale=MV[:, s, 2:3])

        def Le(j):
            kx = ('X', j)
            TT(X[:, j, 0:512], X[:, j, 0:512], LNG[:, 0:512], ALU.mult, ['LN', kx], [('XA', j)], eng='pool')
            TT(X[:, j, 0:512], X[:, j, 0:512], LNB[:, 0:512], ALU.add, ['LN', kx], [('XA', j)], eng='pool')
            TT(X[:, j, 512:1024], X[:, j, 512:1024], LNG[:, 512:1024], ALU.mult, ['LN', kx], [('XB', j)])
            TT(X[:, j, 512:1024], X[:, j, 512:1024], LNB[:, 512:1024], ALU.add, ['LN', kx], [('XB', j)])

        def Lf(j):
            make_XT(j, extra=[('XA', j), ('XB', j)], evac_act=True)
        return [La, Lb, Lc, Ld, Le, Lf]

    def epilogue(pre_stages, W, wkey, lb, SRC=None):
        obank = {}
        if SRC is None:
            SRC = XT

        def O1(j):
            bs = []
            for hf in range(2):
                b = bank(hold=True)
                for k in range(8):
                    MM(ps[b][:], SRC[:, k, j * 128:(j + 1) * 128], W[:, k, hf * 512:(hf + 1) * 512],
                       k == 0, k == 7, [('XT', j), wkey], [('ps', b)])
                bs.append(b)
            obank[j] = bs

        def O2(j):
            for hf in range(2):
                b = obank[j][hf]
                STT(X[:, j, hf * 512:(hf + 1) * 512], X[:, j, hf * 512:(hf + 1) * 512], ALPHA, ps[b][:],
                    ALU.mult, ALU.add, [], [('ps', b), ('X', j)])
                free(b)
        lst = ln_stages(lb)

        def O2La(j):
            O2(j)
            lst[0](j)
        pipeline(range(NT), list(pre_stages) + [O1, O2La] + lst[1:])

    MEMF = sb("MEMF", [128, 2, D], F32)
    LOAD(MEMF[:], mem_d.rearrange("(j p) d -> p j d", p=128), ['MEMF'])
    MEMSET(EPSC[:, 0:1], EPS, ['EPSC'])
    MEMSET(EPSC[:, 1:2], 1.0, ['EPSC'])
    MEMSET(EPSC[:, 2:3], math.log(128.0 ** -0.5), ['EPSC'])
    MEMSET(EPSC[:, 3:4], -math.log(128.0 ** -0.5), ['EPSC'])

    for j in range(NT):
        make_XT(j)
    for hb in range(2):
        for mt in range(2):
            b = bank()
            for c in range(4):
                cc = hb * 4 + c
                TR(ps[b][:, c * 128:(c + 1) * 128], MEMF[:, mt, cc * 128:(cc + 1) * 128], ['MEMF'], [('ps', b)])
            dst = MEMT[:, hb * 4:(hb + 1) * 4, mt * 128:(mt + 1) * 128]
            src = ps[b][:].rearrange("p (c t) -> p c t", c=4)
            P.add('act', lambda dst=dst, src=src: nc.scalar.copy(out=dst, in_=src), (), [('ps', b), 'MEMT'])

    marks = []
    stored = set()

    def mark(name):
        marks.append((name, sum(1 for o in P.ops if o[0] == 'pe')))

    def phase_reset():
        P.barrier()
        cur[0] = phase_base

    def cross_attn(l):
        mark('cross%d' % l)
        phase_reset()
        KMT = sb("KMT", [128, 8, MEM], BF16)
        VM = sb("VM", [128, 2, D], BF16)
        OT = sb("OT", [128, 8, S], BF16)
        part_base = cur[0]
        WS = [sb("WS%d" % i, [128, 8, 512], BF16) for i in range(2)]
        QT = [sb("QT%d" % i, [128, 2, S], BF16) for i in range(2)]
        WQ = [sb("WQ%d" % i, [128, 8, 256], BF16) for i in range(2)]
        PT = [sb("PT%d" % i, [128, 512], BF16) for i in range(4)]
        RS = [sb("RS%d" % i, [128, 512], F32) for i in range(2)]
        WO = sb("WO", [128, 8, D], BF16)
        epi_base = cur[0]
        st = {}
        pti = [0]

        def q_proj(h):
            wq = WQ[h % 2]
            wqk = ('WQ', h % 2)
            qt = QT[h % 2]
            for dc in range(2):
                for t4 in range(4):
                    b = bank()
                    for k in range(8):
                        MM(ps[b][:], wq[:, k, dc * 128:(dc + 1) * 128], XT[:, k, t4 * 512:(t4 + 1) * 512],
                           k == 0, k == 7, [wqk] + XTk(t4 * 4, t4 * 4 + 4), [('ps', b)])
                    ACT(qt[:, dc, t4 * 512:(t4 + 1) * 512], ps[b][:], AF.Identity, [], [('ps', b), ('QT', h % 2, t4)],
                        scale=1.0 / 16.0)

        LOADC(WQ[0][:], wslab(mem_wq[l][:, 0:256]), [('WQ', 0)])
        for s in range(2):
            LOADC(WS[s][:], wslab(mem_wkv[l][:, s * 512:(s + 1) * 512]), [('WS', s)])
        q_proj(0)
        for s in range(4):
            w = WS[s % 2]
            wk = ('WS', s % 2)
            if s >= 2:
                LOADC(w[:], wslab(mem_wkv[l][:, s * 512:(s + 1) * 512]), [wk])
            if s < 2:
                for m2 in range(4):
                    c = s * 4 + m2
                    b = bank()
                    for k in range(8):
                        MM(ps[b][:, 0:MEM], w[:, k, m2 * 128:(m2 + 1) * 128], MEMT[:, k, :], k == 0, k == 7,
                           [wk, 'MEMT'], [('ps', b)])
                    ACT(KMT[:, c, :], ps[b][:, 0:MEM], AF.Copy, [], [('ps', b), 'KMT'])
            else:
                for mt in range(2):
                    b = bank()
                    for k in range(8):
                        MM(ps[b][:], MEMT[:, k, mt * 128:(mt + 1) * 128], w[:, k, :], k == 0, k == 7,
                           [wk, 'MEMT'], [('ps', b)])
                    ACT(VM[:, mt, (s - 2) * 512:(s - 1) * 512], ps[b][:], AF.Copy, [], [('ps', b), 'VM'])

        def CQ(it):
            h, tg = it
            if tg != 0:
                return
            if h >= 1:
                q_proj(h)
            if h + 1 < 4:
                LOADC(WQ[(h + 1) % 2][:], wslab(mem_wq[l][:, (h + 1) * 256:(h + 2) * 256]), [('WQ', (h + 1) % 2)])
            if h == 2:
                for hf in range(2):
                    LOADC(WO[:, :, hf * 512:(hf + 1) * 512], wslab(mem_wo[l][:, hf * 512:(hf + 1) * 512]), ['WO'])

        def C1(it):
            h, tg = it
            qt = QT[h % 2]
            bs = []
            for mt in range(2):
                b = bank(hold=True)
                for dc in range(2):
                    MM(ps[b][:], KMT[:, 2 * h + dc, mt * 128:(mt + 1) * 128], qt[:, dc, tg * 512:(tg + 1) * 512],
                       dc == 0, dc == 1, ['KMT', ('QT', h % 2, tg)], [('ps', b)])
                bs.append(b)
            st[('c1', it)] = bs

        def C2(it):
            pts = []
            for mt in range(2):
                b = st[('c1', it)][mt]
                pi = pti[0] % 4
                pti[0] += 1
                ACT(PT[pi][:], ps[b][:], AF.Exp, [], [('ps', b), ('PT', pi)])
                free(b)
                pts.append(pi)
            st[('c2', it)] = pts

        def C3(it):
            h, tg = it
            pts = st[('c2', it)]
            b = bank(hold=True)
            for mt in range(2):
                MM(ps[b][:], ONESB[:], PT[pts[mt]][:], mt == 0, mt == 1, ['constb', ('PT', pts[mt])], [('ps', b)])
            bo = []
            for dc in range(2):
                b2 = bank(hold=True)
                for mt in range(2):
                    MM(ps[b2][:], VM[:, mt, h * 256 + dc * 128:h * 256 + (dc + 1) * 128], PT[pts[mt]][:],
                       mt == 0, mt == 1, ['VM', ('PT', pts[mt])], [('ps', b2)])
                bo.append(b2)
            st[('c3', it)] = (b, bo)

        def C4(it):
            h, tg = it
            b, bo = st[('c3', it)]
            ri = (h * 4 + tg) % 2
            RECIP(RS[ri][:], ps[b][:], [], [('ps', b), ('RS', ri)])
            for dc in range(2):
                TT(OT[:, 2 * h + dc, tg * 512:(tg + 1) * 512], ps[bo[dc]][:], RS[ri][:], ALU.mult,
                   [('RS', ri)], [('ps', bo[dc]), ('OT', tg)])
            free(b, *bo)
        pipeline([(h, tg) for h in range(4) for tg in range(4)], [CQ, C1, C2, C3, C4])
        P.barrier()
        cur[0] = part_base
        lb = ln_bufs(l, 1)
        obank = {}

        def O1(j):
            bs = []
            for hf in range(2):
                b = bank(hold=True)
                for k in range(8):
                    MM(ps[b][:], OT[:, k, j * 128:(j + 1) * 128], WO[:, k, hf * 512:(hf + 1) * 512],
                       k == 0, k == 7, ['WO'], [('ps', b)])
                bs.append(b)
            obank[j] = bs

        def O2(j):
            for hf in range(2):
                b = obank[j][hf]
                STT(X[:, j, hf * 512:(hf + 1) * 512], X[:, j, hf * 512:(hf + 1) * 512], ALPHA, ps[b][:],
                    ALU.mult, ALU.add, [], [('ps', b), ('X', j)])
                free(b)
        lst = ln_stages(lb)

        def O2La(j):
            O2(j)
            lst[0](j)
        pipeline(range(NT), [O1, O2La] + lst[1:])

    def ffn(l):
        mark('ffn%d' % l)
        phase_reset()
        HT = sb("HT", [128, 22, 1024], BF16)
        WG = [sb("WG%d" % i, [128, 8, 256], BF16) for i in range(2)]
        WU = [sb("WU%d" % i, [128, 8, 256], BF16) for i in range(2)]
        WD = [sb("WD%d" % i, [128, 22, 128], BF16) for i in range(2)]
        SG = [sb("SG%d" % i, [128, 512], F32) for i in range(2)]
        YT = [sb("YT%d" % i, [128, 512], F32) for i in range(2)]
        lb = ln_bufs(l, 2)
        lnst = ln_stages(lb)
        if l == DEPTH - 1:
            def Lstore(j):
                o = LOAD(y_d[j * 128:(j + 1) * 128, :], X[:, j, :], [], r=[('X', j), ('XA', j), ('XB', j)])
                P.final_wait.append(o)
                stored.add(j)
            lnst = lnst + [Lstore]
        cnt = [0]
        wi = 0
        di = [0]
        bg = None
        for half in range(2):
            t0 = half * 1024
            for s in range(11):
                wg, wu = WG[wi % 2], WU[wi % 2]
                kg, ku = ('WG', wi % 2), ('WU', wi % 2)
                wi += 1
                LOADC(wg[:], wslab(ffn_w_gu[l][:, s * 256:(s + 1) * 256]), [kg])
                LOADC(wu[:], wslab(ffn_w_gu[l][:, FFN + s * 256:FFN + (s + 1) * 256]), [ku])
                for m2 in range(2):
                    mc = s * 2 + m2
                    for tg2 in range(2):
                        tt0 = t0 + tg2 * 512
                        xk = XTk(tt0 // 128, tt0 // 128 + 4)
                        bgk = bank()
                        for k in range(8):
                            MM(ps[bgk][:], wg[:, k, m2 * 128:(m2 + 1) * 128], XT[:, k, tt0:tt0 + 512], k == 0, k == 7,
                               [kg] + xk, [('ps', bgk)])
                        bu = bank()
                        for k in range(8):
                            MM(ps[bu][:], wu[:, k, m2 * 128:(m2 + 1) * 128], XT[:, k, tt0:tt0 + 512], k == 0, k == 7,
                               [ku] + xk, [('ps', bu)])
                        si = cnt[0] % 2
                        cnt[0] += 1
                        ACT(SG[si][:], ps[bgk][:], AF.Silu, [], [('ps', bgk), ('SG', si)])
                        TT(HT[:, mc, tg2 * 512:(tg2 + 1) * 512], SG[si][:], ps[bu][:], ALU.mult,
                           [('SG', si)], [('ps', bu), ('HT', tg2)])
                        if bg is not None and (mc * 2 + tg2) % 3 == 0:
                            bg.tick()
            if bg is not None:
                bg.drain()
            st = {}

            def D1(it, half=half):
                m, tg2 = it
                if bgA[0] is not None:
                    bgA[0].tick()
                wd = WD[di[0] % 2]
                kd = ('WD', di[0] % 2)
                di[0] += 1
                LOADC(wd[:], ffn_w_down[l][:, m * 128:(m + 1) * 128].rearrange("(k p) n -> p k n", p=128), [kd])
                b = bank(hold=True)
                for kc in range(22):
                    MM(ps[b][:], wd[:, kc, :], HT[:, kc, tg2 * 512:(tg2 + 1) * 512], kc == 0, kc == 21,
                       [kd, ('HT', tg2)], [('ps', b)])
                st[('d1', it)] = b

            def D2(it, half=half):
                b = st[('d1', it)]
                yi = cnt[0] % 2
                cnt[0] += 1
                ACT(YT[yi][:], ps[b][:], AF.Copy, [], [('ps', b), ('YT', yi)])
                free(b)
                st[('d2', it)] = yi

            def D3(it, half=half):
                yi = st[('d2', it)]
                b2 = bank(hold=True)
                for ts in range(4):
                    TR(ps[b2][:, ts * 128:(ts + 1) * 128], YT[yi][:, ts * 128:(ts + 1) * 128], [('YT', yi)], [('ps', b2)])
                st[('d3', it)] = b2

            def D4(it, half=half):
                m, tg2 = it
                b2 = st[('d3', it)]
                j0 = half * 8 + tg2 * 4
                xs = X[:, j0:j0 + 4, m * 128:(m + 1) * 128]
                STT(xs, xs, ALPHA, ps[b2][:].rearrange("p (a f) -> p a f", a=4), ALU.mult, ALU.add,
                    [], [('ps', b2)] + [('X', j) for j in range(j0, j0 + 4)])
                free(b2)
            bgA = [None]
            pipeline([(m, 0) for m in range(8)], [D1, D2, D3, D4])
            bgA[0] = Pipe(range(half * 8, half * 8 + 4), lnst)
            pipeline([(m, 1) for m in range(8)], [D1, D2, D3, D4])
            bgA[0].drain()
            bgA[0] = None
            bg = Pipe(range(half * 8 + 4, half * 8 + 8), lnst)
        bg.drain()

    def odd_mixer(l):
        jj = l // 2
        lam_init = 0.8 - 0.6 * math.exp(-0.3 * l)
        mark('odd%d' % l)
        phase_reset()
        MT = sb("MT", [128, 8, S], BF16)
        WOUT = sb("WOUT", [128, 8, D], BF16)
        fin_base = cur[0]
        COS = sb("COS", [128, S], BF16)
        SIN = sb("SIN", [128, S], BF16)
        LAMP = sb("LAMP", [128, 4, 64], F32)
        LT = sb("LT", [128, 2, 64], F32)
        LS = sb("LS", [128, 8], F32)
        SUBC = sb("SUBC", [128, 2], F32)
        W5 = [{q: sb("W5_%d_%d" % (i, q), [128, 8, 128], BF16) for q in (0, 2, 4)} for i in range(2)]
        QTH = sb("QTH", [128, S], BF16)
        KTHC = [sb("KTHC%d" % i, [128, S], BF16) for i in range(2)]
        VH = sb("VH", [128, NT, 128], BF16)
        FW = [sb("FW%d" % i, [128, 512], F32) for i in range(6)]
        SQB = sb("SQB", [128, 512], BF16)
        QB = [sb("QB%d" % i, [128, 512], BF16) for i in range(2)]
        PERMB = sb("PERMB", [128, 128], BF16)
        LOADC(PERMB[:], permm_d, ['PERMB'])
        qbi = [0]
        PT = [sb("PTo%d" % i, [128, 512], BF16) for i in range(4)]
        for q in range(4):
            LOADC(COS[:, q * 512:(q + 1) * 512], cosT_d[:, q * 512:(q + 1) * 512], ['ROPE'])
            LOADC(SIN[:, q * 512:(q + 1) * 512], sinT_d[:, q * 512:(q + 1) * 512], ['ROPE'])
        LOAD(LAMP[:], lamp_d[jj].rearrange("p (a b) -> p a b", a=4), ['LAMP'])
        LOAD(SUBC[:, 0:1], subc_d[jj], ['SUBC'])
        TT(LT[:, 0, :], LAMP[:, 0, :], LAMP[:, 1, :], ALU.mult, ['LAMP'], ['LT'])
        TT(LT[:, 1, :], LAMP[:, 2, :], LAMP[:, 3, :], ALU.mult, ['LAMP'], ['LT'])
        P.add('dve', lambda: nc.vector.tensor_reduce(out=LS[:, 0:2], in_=LT[:], axis=AX.X, op=ALU.add), ['LT'], ['LS'])
        ACT(LS[:, 2:4], LS[:, 0:2], AF.Exp, ['LS'], ['LS2'])
        TT(LS[:, 4:5], LS[:, 3:4], LS[:, 2:3], ALU.subtract, ['LS2'], ['LS3'])
        TS(LS[:, 5:6], LS[:, 4:5], -lam_init, None, ALU.add, None, ['LS3'], ['NLAM'])
        NLAM = LS[:, 5:6]
        for c_ in range(2):
            for q in range(4):
                MEMSET(KTHC[c_][:, q * 512:(q + 1) * 512], 0.0, [('KTH', q)], eng='pool')
        ri = [0]
        pti = [0]
        deferred = []

        def tick():
            for d_ in deferred:
                d_[0] -= 1
            while deferred and deferred[0][0] <= 0:
                deferred.pop(0)[1]()

        def fw(i):
            return FW[i], ('FW', i)

        def load_w5(h):
            ws = W5[h % 2]
            wk = [('W5', h % 2, q) for q in range(5)]
            LOADC(ws[0][:], wslab(odd_w_in[jj][:, h * 128:(h + 1) * 128]), [wk[0]])
            LOADC(ws[2][:], wslab(odd_w_in[jj][:, D + h * 128:D + (h + 1) * 128]), [wk[2]])
            LOADC(ws[4][:], wslab(odd_w_in[jj][:, 2 * D + h * 128:2 * D + (h + 1) * 128]), [wk[4]])

        load_w5(0)
        for h in range(8):
            ws = W5[h % 2]
            wk = [('W5', h % 2, q) for q in range(5)]
            pb = (0, 1, 2, 3)
            gst = {}

            def G1(g):
                tick()
                (dst, dkey, wa), tg = g
                sl = slice(tg * 512, (tg + 1) * 512)
                ba = bank(hold=True)
                for k in range(8):
                    MM(ps[ba][:], ws[wa][:, k, :], XT[:, k, sl], k == 0, k == 7, [wk[wa]] + XTk(tg * 4, tg * 4 + 4), [('ps', ba)])
                gst[('a', g)] = ba

            def G2(g):
                ba = gst[('a', g)]
                qi = qbi[0] % 2
                qbi[0] += 1
                ACT(QB[qi][:], ps[ba][:], AF.Copy, [], [('ps', ba), ('QB', qi)])
                gst[('q', g)] = qi

            def G3(g):
                qi = gst[('q', g)]
                bb = bank(hold=True)
                MM(ps[bb][:], PERMB[:], QB[qi][:], True, True, ['PERMB', ('QB', qi)], [('ps', bb)])
                gst[('b', g)] = bb

            def G4(g):
                (dst, dkey, wa), tg = g
                sl = slice(tg * 512, (tg + 1) * 512)
                ba, bb = gst[('a', g)], gst[('b', g)]
                r = ri[0] % 2
                ri[0] += 1
                gst[('r', g)] = r
                (r1, k1), (r2, k2) = fw(4 + r), fw(2 + r)
                TT(r1[:], ps[ba][:], COS[:, sl], ALU.mult, ['ROPE'], [('ps', ba), k1])
                TT(r2[:], ps[bb][:], SIN[:, sl], ALU.mult, ['ROPE'], [('ps', bb), k2])
                free(ba, bb)

            def G5(g):
                (dst, dkey, wa), tg = g
                sl = slice(tg * 512, (tg + 1) * 512)
                r = gst[('r', g)]
                (r1, k1), (r2, k2) = fw(4 + r), fw(2 + r)
                if dst == 'K':
                    for c_ in range(2):
                        pr = slice(c_ * 64, (c_ + 1) * 64)
                        TT(KTHC[c_][pr, sl], r1[pr, :], r2[pr, :], ALU.add, [k1, k2], [(dkey, tg)], eng='pool')
                else:
                    TT(QTH[:, sl], r1[:], r2[:], ALU.add, [k1, k2], [(dkey, tg)], eng='pool')
            pipeline([(qk, tg) for qk in (('K', 'KTH', 2), ('Q', 'QTH', 0)) for tg in range(4)], [G1, G2, G3, G4, G5])
            for j4 in range(4):
                tick()
                b = bank(pb)
                for ts in range(4):
                    j = j4 * 4 + ts
                    for k in range(8):
                        MM(ps[b][:, ts * 128:(ts + 1) * 128], XT[:, k, j * 128:(j + 1) * 128], ws[4][:, k, :],
                           k == 0, k == 7, [wk[4], ('XT', j)], [('ps', b)])
                ACT(VH[:, j4 * 4:(j4 + 1) * 4, :], ps[b][:].rearrange("p (a f) -> p a f", a=4), AF.Copy,
                    [], [('ps', b), 'VH'])
            if h + 1 < 8:
                load_w5(h + 1)
            if h == 6:
                for hf in range(2):
                    LOADC(WOUT[:, :, hf * 512:(hf + 1) * 512], wslab(odd_w_out[jj][:, hf * 512:(hf + 1) * 512]), ['WOUT'])
            st = {}

            def A1(it):
                tick()
                tg, kt, comp = it
                b = bank((0, 1, 2, 3), hold=True)
                MM(ps[b][:], KTHC[comp][:, kt * 128:(kt + 1) * 128], QTH[:, tg * 512:(tg + 1) * 512], True, True,
                   [('KTH', kt // 4), ('QTH', tg)], [('ps', b)])
                st[('a1', it)] = b

            def A2(it):
                b = st[('a1', it)]
                pi = pti[0] % 4
                pti[0] += 1
                ACT(PT[pi][:], ps[b][:], AF.Exp, [], [('ps', b), ('PT', pi)], scale=0.125)
                free(b)
                st[('a2', it)] = pi

            def A3(it, h=h):
                tg, kt, comp = it
                pi = st[('a2', it)]
                bo, bd = 4 + 2 * comp, 5 + 2 * comp
                MM(ps[bo][:], VH[:, kt, :], PT[pi][:], kt == 0, kt == NT - 1, [('PT', pi), 'VH'], [('ps', bo)])
                MM(ps[bd][:], ONESB[:], PT[pi][:], kt == 0, kt == NT - 1, [('PT', pi), 'constb'], [('ps', bd)])
                if not (kt == NT - 1 and comp == 1):
                    return
                (O0, kO0), (D0, kD0), (O1, kO1), (D1, kD1) = fw(0), fw(1), fw(2), fw(3)
                ACT(O0[:], ps[4][:], AF.Copy, [], [('ps', 4), kO0])
                P.add('dve', lambda: nc.vector.tensor_copy(out=D0[:], in_=ps[5][:]), (), [('ps', 5), kD0])
                ACT(O1[:], ps[6][:], AF.Copy, [], [('ps', 6), kO1])
                P.add('dve', lambda: nc.vector.tensor_copy(out=D1[:], in_=ps[7][:]), (), [('ps', 7), kD1])
                RECIP(D0[:], D0[:], [], [kD0])
                TT(O0[:], O0[:], D0[:], ALU.mult, [kD0], [kO0], eng='pool')
                RECIP(D1[:], D1[:], [], [kD1])
                TT(O1[:], O1[:], D1[:], ALU.mult, [kD1], [kO1], eng='pool')
                STT(O0[:], O1[:], NLAM, O0[:], ALU.mult, ALU.add, ['NLAM', kO1], [kO0])

                def tail1():
                    ACT(SQB[:], O0[:], AF.Square, [kO0], ['SQB'])

                def tail2(h=h, tg=tg):
                    bss = bank((0, 1, 2, 3), hold=True)
                    MM(ps[bss][:], ONESB[:], SQB[:], True, True, ['SQB', 'constb'], [('ps', bss)])
                    ACT(D0[:], ps[bss][:], AF.Ln, [], [('ps', bss), kD0], bias=EPSC[:, 0:1], scale=1.0 / 128.0)
                    free(bss)
                    ACT(D0[:], D0[:], AF.Exp, [], [kD0], scale=-0.5)
                    TT(O0[:], O0[:], D0[:], ALU.mult, [kD0], [kO0], eng='pool')
                    TS(MT[:, h, tg * 512:(tg + 1) * 512], O0[:], SUBC[:, 0:1], 1.0 - lam_init, ALU.mult, ALU.mult,
                       [kO0, 'SUBC'], [('MT', tg)])
                deferred.append([22, tail1])
                deferred.append([26, tail2])
            pipeline([(tg, kt, comp) for tg in range(4) for kt in range(NT) for comp in range(2)], [A1, A2, A3], gap=2)
        while deferred:
            deferred.pop(0)[1]()
        mark('oddfin%d' % l)
        P.barrier()
        cur[0] = fin_base
        lb = ln_bufs(l, 0)
        epilogue([], WOUT, 'WOUT', lb, SRC=MT)

    def even_mixer(l):
        jj = l // 2
        mark('even%d' % l)
        phase_reset()
        BO = sb("BO", [128, NT, 512], BF16)
        H = sb("H", [128, NT, 512], F32)
        fin_base = cur[0]
        CW = sb("CW", [128, 8, 5], F32)
        GA = sb("GA", [128, NT, 16], F32)
        GBR = sb("GBR", [128, NT, 16], F32)
        SP = sb("SP", [128, NT, 8], F32)
        UE = sb("UE", [128, NT, 8], F32)
        RE = sb("RE", [128, NT, 8], F32)
        GD = sb("GD", [128, NT, 8], F32)
        WGT = sb("WGT", [128, 8, 16], BF16)
        W3 = [sb("W3_%d" % q, [128, 8, 128], BF16) for q in range(3)]
        head_base = cur[0]
        LOAD(CW[:], conv_d[jj], ['CW'])
        LOAD(GBR[:], gateb_d[jj].rearrange("p (a b) -> p a b", a=NT), ['GBR'])
        for q in range(4):
            MEMSET(H[:, q * 4:(q + 1) * 4, :], 0.0, [('H', j) for j in range(q * 4, q * 4 + 4)], eng='pool')
        BQT = sb("BQT", [128, S], BF16)
        BKTC = [sb("BKTC%d" % i, [128, S], BF16) for i in range(2)]
        BV = sb("BV", [128, NT, 2, 65], BF16)
        BMs = [sb("BM%d" % i, [128, NVAR, 2, 128], BF16) for i in range(2)]
        PN = [sb("PN%d" % i, [128, 512], BF16) for i in range(6)]
        RDN = sb("RDN", [128, 4], F32)
        MEMSET(BV[:, :, :, 64:65], 1.0, ['BV'])
        for c_ in range(2):
            for q in range(4):
                MEMSET(BKTC[c_][:, q * 512:(q + 1) * 512], 0.0, [('BKT', q)], eng='pool')
        pni = [0]
        rdi = [0]
        for cb in range(4):
            wk = [('W3', q) for q in range(3)]
            for q, c0 in enumerate((2064, 2576, 3088)):
                LOADC(W3[q][:], wslab(even_w_in[jj][:, c0 + cb * 128:c0 + (cb + 1) * 128]), [wk[q]])
            for v0 in range(0, NVAR, 4):
                v1 = min(NVAR, v0 + 4)
                LOADC(BMs[cb % 2][:, v0:v1, :, :], nabm_d[jj, cb][:, v0:v1, :, :], [('BM', cb % 2)])
            for (dst, dkey, q, sc) in ((BQT, 'BQT', 0, 0.125), (None, 'BKT', 1, 1.0)):
                for tg in range(4):
                    sl = slice(tg * 512, (tg + 1) * 512)
                    b = bank()
                    for k in range(8):
                        MM(ps[b][:], W3[q][:, k, :], XT[:, k, sl], k == 0, k == 7, [wk[q]] + XTk(tg * 4, tg * 4 + 4), [('ps', b)])
                    if dst is None:
                        for c_ in range(2):
                            pr = slice(c_ * 64, (c_ + 1) * 64)
                            ACT(BKTC[c_][pr, sl], ps[b][pr, :], AF.Copy, [], [('ps', b), (dkey, tg)])
                    else:
                        ACT(dst[:, sl], ps[b][:], AF.Identity, [], [('ps', b), (dkey, tg)], scale=sc)
            for j4 in range(4):
                b = bank()
                for ts in range(4):
                    j = j4 * 4 + ts
                    for k in range(8):
                        MM(ps[b][:, ts * 128:(ts + 1) * 128], XT[:, k, j * 128:(j + 1) * 128], W3[2][:, k, :],
                           k == 0, k == 7, [wk[2], ('XT', j)], [('ps', b)])
                for hh in range(2):
                    ACT(BV[:, j4 * 4:(j4 + 1) * 4, hh, 0:64],
                        ps[b][:].rearrange("p (a f) -> p a f", a=4)[:, :, hh * 64:(hh + 1) * 64], AF.Copy,
                        [], [('ps', b), 'BV'])
            st = {}
            items = [(jt, hh) for jt in range(NT) for hh in range(2)]

            def N1(it, cb=cb):
                jt, hh = it
                lst = _NA_PLAN[jt]
                bs = []
                for g0 in range(0, len(lst), 4):
                    b = bank((0, 1, 2, 3, 4, 5), hold=True)
                    for i, (kt, var) in enumerate(lst[g0:g0 + 4]):
                        reg = ps[b][:, i * 128:(i + 1) * 128]
                        MM(reg, BKTC[hh][:, kt * 128:(kt + 1) * 128], BQT[:, jt * 128:(jt + 1) * 128], True, False,
                           [('BKT', kt // 4), ('BQT', jt // 4)], [('ps', b)])
                        MM(reg, IDB[:], BMs[cb % 2][:, var, hh, :], False, True, ['constb', ('BM', cb % 2)], [('ps', b)])
                    bs.append((b, len(lst[g0:g0 + 4])))
                st[('n1', it)] = bs

            def N2(it):
                out = []
                for (b, n) in st[('n1', it)]:
                    pi = pni[0] % 6
                    pni[0] += 1
                    ACT(PN[pi][:, 0:n * 128], ps[b][:, 0:n * 128], AF.Exp, [], [('ps', b), ('PN', pi)])
                    free(b)
                    out.append((pi, n))
                st[('n2', it)] = out

            def N3(it, cb=cb):
                jt, hh = it
                h = 2 * cb + hh
                lst = _NA_PLAN[jt]
                bo = bank((6, 7), hold=True)
                idx = 0
                for (pi, n) in st[('n2', it)]:
                    for i in range(n):
                        kt = lst[idx][0]
                        MM(ps[bo][:, 0:65], PN[pi][:, i * 128:(i + 1) * 128], BV[:, kt, hh, :], idx == 0, idx == len(lst) - 1,
                           [('PN', pi), 'BV'], [('ps', bo)])
                        idx += 1
                r = rdi[0] % 4
                rdi[0] += 1
                RECIP(RDN[:, r:r + 1], ps[bo][:, 64:65], [], [('ps', bo), ('RDN', r)])
                TS(BO[:, jt, h * 64:(h + 1) * 64], ps[bo][:, 0:64], RDN[:, r:r + 1], None, ALU.mult, None,
                   [('RDN', r)], [('ps', bo), ('BO', jt)])
                free(bo)
            pipeline(items, [N1, N2, N3], gap=1)
        mark('mlstm%d' % l)
        P.barrier()
        cur[0] = head_base
        RAWP = sb("RAWP", [128, S + 4], BF16)
        DG = [sb("DG%d" % i, [128, 5, 128], BF16) for i in range(2)]
        CFG = sb("CFG", [128, 2, 129], F32)
        QCs = [sb("QC%d" % i, [128, S], BF16) for i in range(2)]
        KCs = [sb("KC%d" % i, [128, S], BF16) for i in range(2)]
        KTOKs = [sb("KTOK%d" % i, [128, NT, 128], BF16) for i in range(2)]
        VHs = [sb("VHe%d" % i, [128, NT, 129], BF16) for i in range(2)]
        CF = sb("CF", [128, 2, 129], F32)
        CB = sb("CB", [128, 2, 129], BF16)
        SMT = [sb("SMT%d" % i, [128, 128], BF16) for i in range(8)]
        VP = [sb("VP%d" % i, [128, 129], BF16) for i in range(8)]
        SM = sb("SM", [128, 8, 4], F32)
        GTMP = sb("GTMP", [128, NT, 8], F32)
        IRE = sb("IRE", [128, NT, 8], F32)
        LOADC(WGT[:], wslab(even_w_in[jj][:, 2048:2064]), ['WGT'])
        bg = bank()
        for jt in range(NT):
            for k in range(8):
                MM(ps[bg][:, jt * 16:(jt + 1) * 16], XT[:, k, jt * 128:(jt + 1) * 128], WGT[:, k, :], k == 0, k == 7,
                   ['WGT', ('XT', jt)], [('ps', bg)])
        TT(GA[:], ps[bg][:, 0:256].rearrange("p (a b) -> p a b", a=NT), GBR[:], ALU.add, ['GBR'], [('ps', bg), 'GA'])
        ACT(GTMP[:], GA[:, :, 8:16], AF.Exp, ['GA'], ['GTMP'], scale=-1.0)
        ACT(SP[:], GTMP[:], AF.Ln, ['GTMP'], ['SP'], bias=EPSC[:, 1:2], scale=1.0)
        bc = bank()
        bt = bank()
        for jt in range(NT):
            MM(ps[bc][:, jt * 8:jt * 8 + 4], TRIF[:], SP[:, jt, 0:4], True, True, ['const', 'SP'], [('ps', bc)])
            MM(ps[bc][:, jt * 8 + 4:jt * 8 + 8], TRIB[:], SP[:, jt, 4:8], True, True, ['const', 'SP'], [('ps', bc)])
            MM(ps[bt][:, jt * 8:jt * 8 + 8], ONES[:], SP[:, jt, :], True, True, ['const', 'SP'], [('ps', bt)])
        csv = ps[bc][:, 0:128].rearrange("p (a b) -> p a b", a=NT)
        TT(GTMP[:], GA[:, :, 0:8], csv, ALU.add, ['GA'], [('ps', bc), 'GTMP'])
        ACT(UE[:], GTMP[:], AF.Exp, ['GTMP'], ['UE'])
        ACT(RE[:], csv, AF.Exp, [], [('ps', bc), 'RE'], bias=EPSC[:, 2:3], scale=-1.0)
        ACT(IRE[:], csv, AF.Exp, [], [('ps', bc), 'RE'], bias=EPSC[:, 3:4], scale=1.0)
        ACT(GD[:], ps[bt][:, 0:128].rearrange("p (a b) -> p a b", a=NT), AF.Exp, [], [('ps', bt), 'GD'], scale=-1.0)
        for i_ in range(2):
            MEMSET(VHs[i_][:, :, 128:129], 1.0, [('VH', i_)])
        MEMSET(RAWP[:, 0:2], 0.0, [('RAW', -1)])
        MEMSET(RAWP[:, S + 2:S + 4], 0.0, [('RAW', 4)])
        rot = [0]
        dgi = [0]
        def proj_chunks(h):
            hp = h % 2
            QC, KC, KTOK, VH = QCs[hp], KCs[hp], KTOKs[hp], VHs[hp]
            kQ, kK, kT, kV = ('QC', hp), ('KC', hp), ('KTOK', hp), ('VH', hp)
            wk = [('W3', q) for q in range(3)]
            ch = []

            def c_load():
                for q, c0 in enumerate((0, 512, 1024)):
                    LOADC(W3[q][:], wslab(even_w_in[jj][:, c0 + h * 128:c0 + (h + 1) * 128]), [wk[q]])
            ch.append(c_load)
            for (dst, dkey, q, cch) in ((QC, kQ, 0, h), (KC, kK, 1, 4 + h)):
                cell = {}

                def c_dg(cch=cch, cell=cell):
                    cell['dg'] = DG[dgi[0] % 2]
                    cell['dgk'] = ('DG', dgi[0] % 2)
                    dgi[0] += 1
                    for kk in range(5):
                        TS(cell['dg'][:, kk, :], IDB[:], CW[:, cch, kk:kk + 1], None, ALU.mult, None, ['constb', 'CW'], [cell['dgk']])
                ch.append(c_dg)
                for tg in range(4):
                    def c_proj(tg=tg, q=q):
                        sl = slice(tg * 512, (tg + 1) * 512)
                        b = bank()
                        for k in range(8):
                            MM(ps[b][:], W3[q][:, k, :], XT[:, k, sl], k == 0, k == 7, [wk[q]] + XTk(tg * 4, tg * 4 + 4), [('ps', b)])
                        ACT(RAWP[:, 2 + tg * 512:2 + (tg + 1) * 512], ps[b][:], AF.Copy, [], [('ps', b), ('RAW', tg)])
                    ch.append(c_proj)
                for tg in range(4):
                    def c_conv(tg=tg, cell=cell, dst=dst, dkey=dkey):
                        b = bank()
                        for kk in range(5):
                            MM(ps[b][:], cell['dg'][:, kk, :], RAWP[:, tg * 512 + kk:tg * 512 + kk + 512], kk == 0, kk == 4,
                               [cell['dgk'], ('RAW', tg - 1), ('RAW', tg), ('RAW', tg + 1)], [('ps', b)])
                        ACT(dst[:, tg * 512:(tg + 1) * 512], ps[b][:], AF.Silu, [], [('ps', b), dkey])
                    ch.append(c_conv)
            for j4 in range(4):
                def c_v(j4=j4):
                    b = bank()
                    for ts in range(4):
                        j = j4 * 4 + ts
                        for k in range(8):
                            MM(ps[b][:, ts * 128:(ts + 1) * 128], XT[:, k, j * 128:(j + 1) * 128], W3[2][:, k, :],
                               k == 0, k == 7, [wk[2], ('XT', j)], [('ps', b)])
                    ACT(VH[:, j4 * 4:(j4 + 1) * 4, 0:128], ps[b][:].rearrange("p (a f) -> p a f", a=4), AF.Copy,
                        [], [('ps', b), kV])
                ch.append(c_v)

                def c_kt(j4=j4):
                    b = bank()
                    for ts in range(4):
                        j = j4 * 4 + ts
                        MM(ps[b][:, ts * 128:(ts + 1) * 128], KC[:, j * 128:(j + 1) * 128], IDB[:], True, True,
                           [kK, 'constb'], [('ps', b)])
                    ACT(KTOK[:, j4 * 4:(j4 + 1) * 4, :], ps[b][:].rearrange("p (a f) -> p a f", a=4), AF.Copy, [], [('ps', b), kT])
                ch.append(c_kt)
            return ch

        for c_ in proj_chunks(0):
            c_()
        for h in range(4):
            hp = h % 2
            QC, KC, KTOK, VH = QCs[hp], KCs[hp], KTOKs[hp], VHs[hp]
            kQ, kK, kT, kV = ('QC', hp), ('KC', hp), ('KTOK', hp), ('VH', hp)
            nxt = proj_chunks(h + 1) if h + 1 < 4 else []
            MEMSET(CF[:], 0.0, [('CF', 0), ('CF', 1)])
            MEMSET(CB[:], 0.0, [('CB', 0), ('CB', 1)])
            MEMSET(CFG[:], 0.0, [('CFG', 0), ('CFG', 1)])
            st = {}

            def geo(it, h=h):
                step, d = it
                c = step if d == 0 else NT - 1 - step
                return c, d * 4 + h, slice(c * 128, (c + 1) * 128)

            def MA(it):
                if nxt:
                    nxt.pop(0)()
                c, col, tsl = geo(it)
                r = rot[0] % 8
                rot[0] += 1
                st[('r', it)] = r
                b1 = bank(hold=True)
                MM(ps[b1][:, 0:128], KC[:, tsl], QC[:, tsl], True, True, [kK, kQ], [('ps', b1)])
                st[('b1', it)] = b1
                ACT(VP[r][:], VH[:, c, :], AF.Identity, [kV, 'UE'], [('VP', r)], scale=UE[:, c, col:col + 1])

            def MB(it):
                d = it[1]
                r = st[('r', it)]
                b1 = st[('b1', it)]
                TT(SMT[r][:], ps[b1][:, 0:128], (TRIF if d == 0 else TRIB)[:], ALU.mult, ['const'],
                   [('ps', b1), ('SMT', r)])
                free(b1)

            def MC(it):
                c, col, tsl = geo(it)
                d = it[1]
                r = st[('r', it)]
                b3 = bank(hold=True)
                MM(ps[b3][:, 0:129], KTOK[:, c, :], VP[r][:], True, True, [kT, ('VP', r)], [('ps', b3)])
                b2 = bank(hold=True)
                MM(ps[b2][:, 0:129], SMT[r][:], VP[r][:], True, False, [('SMT', r), ('VP', r)], [('ps', b2)])
                MM(ps[b2][:, 0:129], QC[:, tsl], CB[:, d, :], False, True, [kQ, ('CB', d)], [('ps', b2)])
                st[('b2', it)] = (b2, b3)

            def MD1(it):
                c, col, tsl = geo(it)
                step, d = it
                b2, b3 = st[('b2', it)]
                g = GD[:, c, col:col + 1]
                STT(CB[:, d, :], ps[b3][:, 0:129], g, CFG[:, d, :], ALU.mult, ALU.add, ['GD', ('CFG', d)], [('ps', b3), ('CB', d)])
                STT(CF[:, d, :], ps[b3][:, 0:129], g, CFG[:, d, :], ALU.mult, ALU.add, ['GD', ('CFG', d)], [('ps', b3), ('CF', d)])
                free(b3)
                if step + 1 < NT:
                    cn, coln, _ = geo((step + 1, d))
                    TS(CFG[:, d, :], CF[:, d, :], GD[:, cn, coln:coln + 1], None, ALU.mult, None, [('CF', d), 'GD'], [('CFG', d)],
                       eng='pool')

            def MD2(it):
                c, col, tsl = geo(it)
                r = st[('r', it)]
                b2, b3 = st[('b2', it)]
                ACT(SM[:, r, 3:4], ps[b2][:, 128:129], AF.Abs, [], [('ps', b2), ('SM0', r)])

            def MD3(it, h=h):
                c, col, tsl = geo(it)
                r = st[('r', it)]
                b2, b3 = st[('b2', it)]
                TS(SM[:, r, 0:1], SM[:, r, 3:4], IRE[:, c, col:col + 1], None, ALU.max, None, [('SM0', r), 'RE'], [('SM', r)])
                RECIP(SM[:, r, 2:3], SM[:, r, 0:1], [('SM', r)], [('SM2', r)])
                hs = H[:, c, h * 128:(h + 1) * 128]
                STT(hs, ps[b2][:, 0:128], SM[:, r, 2:3], hs, ALU.mult, ALU.add, [('SM2', r)], [('ps', b2), ('H', c)])
                free(b2)
            pipeline([(step, d) for step in range(NT) for d in range(2)], [MA, MB, MC, MD1, MD2, MD3])
            while nxt:
                nxt.pop(0)()
        mark('evfin%d' % l)
        P.barrier()
        cur[0] = fin_base
        WOG = sb("WOG", [128, 8, 512], BF16)
        WOUT = sb("WOUTe", [128, 8, D], BF16)
        SGE = [sb("SGE%d" % i, [128, 512], F32) for i in range(2)]
        AO = [sb("AO%d" % i, [128, D], F32) for i in range(2)]
        lb = ln_bufs(l, 0)
        LOADC(WOG[:], wslab(even_w_in[jj][:, 1536:2048]), ['WOG'])
        for hf in range(2):
            LOADC(WOUT[:, :, hf * 512:(hf + 1) * 512], wslab(even_w_out[jj][:, hf * 512:(hf + 1) * 512]), ['WOUT'])
        st = {}

        def E1(j):
            b = bank(hold=True)
            for k in range(8):
                MM(ps[b][:], XT[:, k, j * 128:(j + 1) * 128], WOG[:, k, :], k == 0, k == 7, [('XT', j), 'WOG'], [('ps', b)])
            st[j] = b

        def E2(j):
            b = st[j]
            i = j % 2
            ACT(SGE[i][:], ps[b][:], AF.Sigmoid, [], [('ps', b), ('SGE', i)])
            free(b)
            TT(AO[i][:, 0:512], SGE[i][:], H[:, j, :], ALU.mult, [('SGE', i), ('H', j)], [('AO', i)])
            ACT(AO[i][:, 512:1024], BO[:, j, :], AF.Copy, [('BO', j)], [('AO', i)])

        def E3(j):
            i = j % 2
            make_XT(j, AO[i], ('AO', i))
        epilogue([E1, E2, E3], WOUT, 'WOUT', lb)

    n = 0
    for l in range(DEPTH):
        for sub in range(3):
            if n >= nsub:
                break
            if sub == 0:
                (even_mixer if l % 2 == 0 else odd_mixer)(l)
            elif sub == 1:
                cross_attn(l)
            else:
                ffn(l)
            n += 1
    P.barrier()
    yv = y_d.rearrange("(j p) d -> p j d", p=128)
    for q in range(4):
        if all(j in stored for j in range(q * 4, q * 4 + 4)):
            continue
        o = LOAD(yv[:, q * 4:(q + 1) * 4, :], X[:, q * 4:(q + 1) * 4, :], [], r=[('X', j) for j in range(q * 4, q * 4 + 4)])
        P.final_wait.append(o)
    mark('end')
    stats = P.emit()
    stats['marks'] = marks
    return nc, stats, list(declared.keys())


_CACHE = {}


def run(inputs, nsub=3 * DEPTH):
    inp = {k: np.ascontiguousarray(np.asarray(v, dtype=np.float32)) for k, v in inputs.items()}
    c = host_prep(inp)
    if nsub not in _CACHE:
        _CACHE[nsub] = build(nsub)
    nc, stats, names = _CACHE[nsub]
    allsrc = dict(inp)
    allsrc.update(c)

    def fetch(name):
        if name in allsrc and name not in ('x', 'mem'):
            return allsrc[name]
        base, idx = name, []
        while base not in allsrc:
            base, _, t = base.rpartition('_')
            idx.insert(0, int(t))
        return np.ascontiguousarray(allsrc[base][tuple(idx)])

    shared = {n: fetch(n) for n in names if n not in ('x', 'mem')}
    in_maps = []
    for core in range(8):
        m = dict(shared)
        m['x'] = np.ascontiguousarray(inp['x'][core])
        m['mem'] = np.ascontiguousarray(inp['mem'][core])
        in_maps.append(m)
    res = run_bass_kernel_spmd(nc, in_maps, core_ids=list(range(8)))
    return np.stack([np.asarray(res.results[i]['y'], dtype=np.float32) for i in range(8)], axis=0)


def kernel(**inputs):
    return run(inputs)
```

```python
import math
import numpy as np
import concourse.bass as bass
import concourse.mybir as mybir
from concourse.bass_utils import run_bass_kernel_spmd

F32 = mybir.dt.float32
BF16 = mybir.dt.bfloat16
AF = mybir.ActivationFunctionType
ALU = mybir.AluOpType
AX = mybir.AxisListType

D = 1024
S = 2048
NT = 16
DEPTH = 4
MEM = 256
FFN = 2816
ALPHA = (2 * DEPTH) ** 0.25
EPS = 1e-5
EVEN_IN = 3600
NEG = -30000.0


class Prog:
    NSLOT = 8

    def __init__(self, nc):
        self.nc = nc
        self.ops = []
        self.lastw = {}
        self.readers = {}
        self.pending_barrier = None
        self.final_wait = []
        self.eng_obj = {'pe': nc.tensor, 'act': nc.scalar, 'dve': nc.vector,
                        'pool': nc.gpsimd, 'sp': nc.sync}

    def add(self, eng, fn, reads=(), writes=(), dma=False):
        i = len(self.ops)
        deps = set()
        for r in reads:
            if r in self.lastw:
                deps.add(self.lastw[r])
        for w in writes:
            if w in self.lastw:
                deps.add(self.lastw[w])
            deps.update(self.readers.get(w, {}).values())
        if self.pending_barrier and eng in self.pending_barrier:
            deps.update(self.pending_barrier.pop(eng))
        if eng == 'pe' and not dma:
            deps = set(p for p in deps if not (self.ops[p][0] == 'pe' and not self.ops[p][3]))
        for r in reads:
            rd = self.readers.setdefault(r, {})
            if dma:
                rd[(eng, i)] = i
            else:
                rd[eng] = i
        for w in writes:
            self.lastw[w] = i
            self.readers[w] = {}
        deps.discard(i)
        self.ops.append([eng, fn, deps, dma])
        return i

    def barrier(self):
        last = {}
        dmas = {}
        for i, op in enumerate(self.ops):
            if op[3]:
                dmas.setdefault(op[0], []).append(i)
            else:
                last[op[0]] = i
        deps = set(last.values())
        for q, lst in dmas.items():
            deps.update(lst[-self.NSLOT:])
        self.pending_barrier = {e: set(deps) for e in self.eng_obj}
        self.lastw = {}
        self.readers = {}

    def emit(self):
        nc = self.nc
        ops = self.ops
        flagged = set()
        for op in ops:
            for p in op[2]:
                if not ops[p][3]:
                    flagged.add(p)
        engsem = {e: nc.alloc_semaphore("sem_" + e) for e in self.eng_obj}
        dmasem = {e: [nc.alloc_semaphore("dsem_%s_%d" % (e, k)) for k in range(self.NSLOT)]
                  for e in ('sp', 'pool')}
        cnt = {e: 0 for e in self.eng_obj}
        dcnt = {e: 0 for e in self.eng_obj}
        semval = {}
        waited = {e: {} for e in self.eng_obj}
        nwaits = 0
        for i, (eng, fn, deps, dma) in enumerate(ops):
            E = self.eng_obj[eng]
            need = {}
            for p in deps:
                sem, v = semval[p]
                key = id(sem)
                if key not in need or need[key][1] < v:
                    need[key] = (sem, v)
            if dma:
                n = dcnt[eng]
                slot = dmasem[eng][n % self.NSLOT]
                prev = 16 * (n // self.NSLOT)
                if prev > 0:
                    key = id(slot)
                    if key not in need or need[key][1] < prev:
                        need[key] = (slot, prev)
            for key, (sem, v) in need.items():
                if waited[eng].get(key, 0) >= v:
                    continue
                E.wait_ge(sem, v)
                nwaits += 1
                waited[eng][key] = v
            inst = fn()
            if dma:
                inst.then_inc(slot, 16)
                semval[i] = (slot, 16 * (n // self.NSLOT + 1))
                dcnt[eng] += 1
            elif i in flagged:
                cnt[eng] += 1
                inst.then_inc(engsem[eng], 1)
                semval[i] = (engsem[eng], cnt[eng])
        for i in self.final_wait:
            sem, v = semval[i]
            nc.sync.wait_ge(sem, v)
        return dict(n_ops=len(ops), n_waits=nwaits, cnt=cnt, dcnt=dcnt)


def na_tables():
    R, W, NR, NCW = 32, 64, 8, 16
    rows = np.arange(R)
    row_start = np.clip(rows - NR // 2, 0, R - NR)
    cols = np.arange(W)
    col_start = np.clip(cols - NCW // 2, 0, W - NCW)
    tok_r = np.arange(S) // W
    tok_c = np.arange(S) % W
    variants = []
    vkey = {}
    plan = []
    for jt in range(NT):
        qt = np.arange(jt * 128, (jt + 1) * 128)
        qr, qc = tok_r[qt], tok_c[qt]
        lst = []
        for kt in range(NT):
            k = np.arange(kt * 128, (kt + 1) * 128)
            kr, kc = tok_r[k], tok_c[k]
            okr = (kr[:, None] >= row_start[qr][None, :]) & (kr[:, None] < row_start[qr][None, :] + NR)
            okc = (kc[:, None] >= col_start[qc][None, :]) & (kc[:, None] < col_start[qc][None, :] + NCW)
            ok = okr & okc
            if not ok.any():
                continue
            rel_r = kr[:, None] - qr[None, :] + (NR - 1)
            rel_c = np.clip(kc[:, None] - qc[None, :] + (NCW - 1), 0, 2 * NCW - 2)
            idx = np.where(ok, rel_r * (2 * NCW - 1) + rel_c, -1).astype(np.int32)
            key = idx.tobytes()
            if key not in vkey:
                vkey[key] = len(variants)
                variants.append(idx)
            lst.append((kt, vkey[key]))
        plan.append(lst)
    return variants, plan


_NA_VARIANTS, _NA_PLAN = na_tables()
NVAR = len(_NA_VARIANTS)


def host_prep(inp):
    f = np.float32
    c = {}
    c['ident'] = np.eye(128, dtype=f)
    c['trif'] = np.triu(np.ones((128, 128), f))
    c['trib'] = np.tril(np.ones((128, 128), f))
    c['ones'] = np.ones((128, 128), f)
    d = 64
    inv_freq = 10000.0 ** (-np.arange(0, d, 2, dtype=np.float64) / d)
    ang = np.arange(S, dtype=np.float64)[None, :] * inv_freq[:, None]
    p = np.arange(128)
    cosT = np.cos(ang)[p % 32, :]
    sign = np.where((p % 64) < 32, -1.0, 1.0)[:, None]
    sinT = np.sin(ang)[p % 32, :] * sign
    c['cosT'] = cosT.astype(f)
    c['sinT'] = sinT.astype(f)
    i = np.arange(2048)
    partner = (i // 64) * 64 + ((i % 64) + 32) % 64
    pm = np.zeros((128, 128), f)
    cc_ = np.arange(128)
    pm[(cc_ // 64) * 64 + ((cc_ % 64) + 32) % 64, cc_] = 1.0
    c['permm'] = pm
    cw = inp['even_conv_w']
    c['conv_w'] = np.ascontiguousarray(cw.reshape(2, 5, 8, 128).transpose(0, 3, 2, 1))
    gb = inp['even_gate_b']
    c['gate_b'] = np.ascontiguousarray(np.broadcast_to(np.tile(gb, (1, NT))[:, None, :], (2, 128, NT * 16)))
    rpb = inp['even_rpb']
    bm = np.full((2, 8, NVAR, 128, 128), NEG, f)
    for v, idx in enumerate(_NA_VARIANTS):
        ok = idx >= 0
        g = rpb.reshape(2, 8, -1)[:, :, np.where(ok, idx, 0)]
        bm[:, :, v] = np.where(ok[None, None], g, f(NEG))
    bm = bm.reshape(2, 4, 2, NVAR, 128, 128).transpose(0, 1, 4, 3, 2, 5)
    c['na_bm'] = np.ascontiguousarray(bm)
    c['ln_g'] = np.ascontiguousarray(np.broadcast_to(inp['ln_g'][:, :, None, :], (4, 3, 128, D)))
    c['ln_b'] = np.ascontiguousarray(np.broadcast_to(inp['ln_b'][:, :, None, :], (4, 3, 128, D)))
    c['subln_g'] = np.ascontiguousarray(np.broadcast_to(np.tile(inp['odd_subln_g'], (1, 8))[:, None, :], (2, 128, D)))
    c['subln_c'] = np.ascontiguousarray(inp['odd_subln_g'].reshape(2, 128, 1))
    c['lam_p'] = np.ascontiguousarray(np.broadcast_to(inp['odd_lambda'].reshape(2, 1, 256), (2, 128, 256)))
    return c


def build(nsub=3 * DEPTH):
    nc = bass.Bass("TRN2", target_bir_lowering=False)
    P = Prog(nc)

    declared = {}

    def din(name, shape):
        if name not in declared:
            declared[name] = nc.dram_tensor(name, list(shape), F32, kind="ExternalInput").ap()
        return declared[name]

    class LW:
        def __init__(self, name, shape):
            self.name, self.shape = name, shape

        def __getitem__(self, l):
            if isinstance(l, tuple):
                return din("%s_%s" % (self.name, "_".join(str(i) for i in l)), self.shape)
            return din("%s_%d" % (self.name, l), self.shape)

    x_d = din("x", [S, D])
    mem_d = din("mem", [MEM, D])
    even_w_in = LW("even_w_in", [D, EVEN_IN])
    even_w_out = LW("even_w_out", [D, D])
    odd_w_in = LW("odd_w_in", [D, 3 * D])
    odd_w_perm = LW("odd_w_perm", [D, 2 * D])
    odd_w_out = LW("odd_w_out", [D, D])
    mem_wq = LW("mem_wq", [D, D])
    mem_wkv = LW("mem_wkv", [D, 2 * D])
    mem_wo = LW("mem_wo", [D, D])
    ffn_w_gu = LW("ffn_w_gu", [D, 2 * FFN])
    ffn_w_down = LW("ffn_w_down", [FFN, D])
    ident_d = din("ident", [128, 128])
    trif_d = din("trif", [128, 128])
    trib_d = din("trib", [128, 128])
    ones_d = din("ones", [128, 128])
    permm_d = din("permm", [128, 128])
    cosT_d = din("cosT", [128, S])
    sinT_d = din("sinT", [128, S])
    conv_d = LW("conv_w", [128, 8, 5])
    gateb_d = LW("gate_b", [128, NT * 16])
    nabm_d = LW("na_bm", [128, NVAR, 2, 128])
    lng_d = LW("ln_g", [128, D])
    lnb_d = LW("ln_b", [128, D])
    subg_d = LW("subln_g", [128, D])
    subc_d = LW("subln_c", [128, 1])
    lamp_d = LW("lam_p", [128, 256])
    y_d = nc.dram_tensor("y", [S, D], F32, kind="ExternalOutput").ap()

    lo, hi = nc.bump_sbuf(212800)
    cur = [lo]

    def sb(name, shape, dt):
        nbytes = int(np.prod(shape[1:])) * (4 if dt == F32 else 2)
        nbytes = (nbytes + 31) // 32 * 32
        assert cur[0] + nbytes <= hi, (name, cur[0] + nbytes - hi)
        t = nc.alloc_sbuf_tensor_at(name + "_%d" % len(P.ops), list(shape), dt, offset=cur[0])
        cur[0] += nbytes
        return t

    ps = [nc.alloc_psum_tensor("psb%d" % i, [128, 512], F32) for i in range(8)]
    rr = [0]

    held = set()

    def bank(pool=(0, 1, 2, 3, 4, 5, 6, 7), hold=False):
        for _ in range(len(pool)):
            b = pool[rr[0] % len(pool)]
            rr[0] += 1
            if b not in held:
                if hold:
                    held.add(b)
                return b
        raise AssertionError("no free PSUM bank in pool %r (held %r)" % (pool, held))

    def free(*bs):
        for b in bs:
            held.discard(b)

    def MM(out, lhsT, rhs, start, stop, r, w, skip=False):
        P.add('pe', lambda: nc.tensor.matmul(out, lhsT=lhsT, rhs=rhs, start=start, stop=stop,
                                             skip_group_check=skip), r, w)

    def TR(out, in_, r, w):
        P.add('pe', lambda: nc.tensor.transpose(out=out, in_=in_, identity=ID[:]), list(r) + ['const'], w)

    def ACT(out, in_, func, r, w, bias=None, scale=None):
        kw = {}
        if bias is not None:
            kw['bias'] = bias
        if scale is not None:
            kw['scale'] = scale
        P.add('act', lambda: nc.scalar.activation(out=out, in_=in_, func=func, **kw), r, w)

    def TS(out, in0, s1, s2, op0, op1, r, w, eng='dve'):
        e = nc.vector if eng == 'dve' else nc.gpsimd
        if op1 is None:
            P.add(eng, lambda: e.tensor_scalar(out=out, in0=in0, scalar1=s1, scalar2=None, op0=op0), r, w)
        else:
            P.add(eng, lambda: e.tensor_scalar(out=out, in0=in0, scalar1=s1, scalar2=s2, op0=op0, op1=op1), r, w)

    def TT(out, in0, in1, op, r, w, eng='dve'):
        e = nc.vector if eng == 'dve' else nc.gpsimd
        P.add(eng, lambda: e.tensor_tensor(out=out, in0=in0, in1=in1, op=op), r, w)

    def STT(out, in0, scalar, in1, op0, op1, r, w):
        P.add('dve', lambda: nc.vector.scalar_tensor_tensor(out=out, in0=in0, scalar=scalar, in1=in1,
                                                            op0=op0, op1=op1), r, w)

    def RECIP(out, in_, r, w):
        P.add('dve', lambda: nc.vector.reciprocal(out=out, in_=in_), r, w)

    def MEMSET(ap, val, w, eng='dve'):
        e = nc.vector if eng == 'dve' else nc.gpsimd
        P.add(eng, lambda: e.memset(ap, val), (), w)

    def LOAD(out, in_, w, r=()):
        return P.add('sp', lambda: nc.sync.dma_start(out=out, in_=in_), r, w, dma=True)

    def LOADC(out, in_, w, r=()):
        return P.add('pool', lambda: nc.gpsimd.dma_start(out=out, in_=in_), r, w, dma=True)

    def wslab(src2d):
        return src2d.rearrange("(k p) n -> p k n", p=128)

    X = sb("X", [128, NT, D], F32)
    XT = sb("XT", [128, 8, S], BF16)
    MEMT = sb("MEMT", [128, 8, MEM], BF16)
    ID = sb("ID", [128, 128], F32)
    IDB = sb("IDB", [128, 128], BF16)
    TRIF = sb("TRIF", [128, 128], F32)
    TRIB = sb("TRIB", [128, 128], F32)
    ONES = sb("ONES", [128, 128], F32)
    ONESB = sb("ONESB", [128, 128], BF16)
    EPSC = sb("EPSC", [128, 4], F32)
    phase_base = cur[0]

    def XTk(j0, j1):
        return [('XT', j) for j in range(j0, j1)]

    def pipeline(items, stages, gap=1):
        items = list(items)
        n, ns = len(items), len(stages)
        for step in range(n + (ns - 1) * gap):
            for s_ in range(ns - 1, -1, -1):
                i = step - s_ * gap
                if 0 <= i < n:
                    stages[s_](items[i])

    class Pipe:
        def __init__(self, items, stages):
            self.items, self.stages = list(items), stages
            self.k = 0
            self.nsteps = len(self.items) + len(stages) - 1

        def tick(self):
            if self.k >= self.nsteps:
                return False
            for s_ in range(len(self.stages) - 1, -1, -1):
                i = self.k - s_
                if 0 <= i < len(self.items):
                    self.stages[s_](self.items[i])
            self.k += 1
            return True

        def drain(self):
            while self.tick():
                pass

    LOAD(ID[:], ident_d, ['const'])
    LOAD(TRIF[:], trif_d, ['const'])
    LOAD(TRIB[:], trib_d, ['const'])
    LOAD(ONES[:], ones_d, ['const'])
    LOADC(IDB[:], ident_d, ['constb'])
    LOADC(ONESB[:], ones_d, ['constb'])
    xv = x_d.rearrange("(j p) d -> p j d", p=128)
    for q in range(4):
        LOAD(X[:, q * 4:(q + 1) * 4, :], xv[:, q * 4:(q + 1) * 4, :], [('X', j) for j in range(q * 4, q * 4 + 4)])

    def make_XT(j, src=None, skey=None, extra=(), evac_act=False):
        if src is None:
            src, skey = X[:, j, :], ('X', j)
        for hb in range(2):
            b = bank()
            for c in range(4):
                cc = hb * 4 + c
                TR(ps[b][:, c * 128:(c + 1) * 128], src[:, cc * 128:(cc + 1) * 128], [skey] + list(extra), [('ps', b)])
            dst = XT[:, hb * 4:(hb + 1) * 4, j * 128:(j + 1) * 128]
            srcp = ps[b][:].rearrange("p (c t) -> p c t", c=4)
            if hb == 0 or evac_act:
                P.add('act', lambda dst=dst, srcp=srcp: nc.scalar.copy(out=dst, in_=srcp), (), [('ps', b), ('XT', j)])
            else:
                P.add('dve', lambda dst=dst, srcp=srcp: nc.vector.tensor_copy(out=dst, in_=srcp), (), [('ps', b), ('XT', j)])

    def ln_bufs(l, i):
        lb = dict(STAT=sb("STAT", [128, 8, 2, 6], F32), MV=sb("MV", [128, 8, 4], F32),
                  LNG=sb("LNG", [128, D], F32), LNB=sb("LNB", [128, D], F32))
        LOAD(lb['LNG'][:], lng_d[l, i], ['LN'])
        LOAD(lb['LNB'][:], lnb_d[l, i], ['LN'])
        return lb

    def ln_stages(lb):
        STAT, MV, LNG, LNB = lb['STAT'], lb['MV'], lb['LNG'], lb['LNB']

        def La(j):
            s = j % 8
            kx = ('X', j)
            st = ('STAT', s)
            for hh in range(2):
                a = STAT[:, s, hh, :]
                src = X[:, j, hh * 512:(hh + 1) * 512]
                P.add('dve', lambda a=a, src=src: nc.vector.bn_stats(out=a, in_=src), [kx], [st])
            mv = MV[:, s, 0:2]
            stv = STAT[:, s, :, :].rearrange("p a b -> p (a b)")
            P.add('dve', lambda: nc.vector.bn_aggr(out=mv, in_=stv), [st], [('MV', s)])

        def Lb(j):
            s = j % 8
            ACT(MV[:, s, 2:3], MV[:, s, 1:2], AF.Ln, [('MV', s)], [('MV2', s)], bias=EPSC[:, 0:1], scale=1.0)
            ACT(MV[:, s, 2:3], MV[:, s, 2:3], AF.Exp, [], [('MV2', s)], scale=-0.5)

        def Lc(j):
            s = j % 8
            TS(MV[:, s, 3:4], MV[:, s, 0:1], MV[:, s, 2:3], -1.0, ALU.mult, ALU.mult, [('MV', s), ('MV2', s)], [('MV3', s)])

        def Ld(j):
            s = j % 8
            kx = ('X', j)
            ACT(X[:, j, :], X[:, j, :], AF.Identity, [('MV2', s), ('MV3', s)], [kx], bias=MV[:, s, 3:4], scale=MV[:, s, 2:3])

        def Le(j):
            kx = ('X', j)
            TT(X[:, j, 0:512], X[:, j, 0:512], LNG[:, 0:512], ALU.mult, ['LN', kx], [('XA', j)], eng='pool')
            TT(X[:, j, 0:512], X[:, j, 0:512], LNB[:, 0:512], ALU.add, ['LN', kx], [('XA', j)], eng='pool')
            TT(X[:, j, 512:1024], X[:, j, 512:1024], LNG[:, 512:1024], ALU.mult, ['LN', kx], [('XB', j)])
            TT(X[:, j, 512:1024], X[:, j, 512:1024], LNB[:, 512:1024], ALU.add, ['LN', kx], [('XB', j)])

        def Lf(j):
            make_XT(j, extra=[('XA', j), ('XB', j)], evac_act=True)
        return [La, Lb, Lc, Ld, Le, Lf]

    def epilogue(pre_stages, W, wkey, lb, SRC=None):
        obank = {}
        if SRC is None:
            SRC = XT

        def O1(j):
            bs = []
            for hf in range(2):
                b = bank(hold=True)
                for k in range(8):
                    MM(ps[b][:], SRC[:, k, j * 128:(j + 1) * 128], W[:, k, hf * 512:(hf + 1) * 512],
                       k == 0, k == 7, [('XT', j), wkey], [('ps', b)])
                bs.append(b)
            obank[j] = bs

        def O2(j):
            for hf in range(2):
                b = obank[j][hf]
                STT(X[:, j, hf * 512:(hf + 1) * 512], X[:, j, hf * 512:(hf + 1) * 512], ALPHA, ps[b][:],
                    ALU.mult, ALU.add, [], [('ps', b), ('X', j)])
                free(b)
        lst = ln_stages(lb)

        def O2La(j):
            O2(j)
            lst[0](j)
        pipeline(range(NT), list(pre_stages) + [O1, O2La] + lst[1:])

    MEMF = sb("MEMF", [128, 2, D], F32)
    LOAD(MEMF[:], mem_d.rearrange("(j p) d -> p j d", p=128), ['MEMF'])
    MEMSET(EPSC[:, 0:1], EPS, ['EPSC'])
    MEMSET(EPSC[:, 1:2], 1.0, ['EPSC'])
    MEMSET(EPSC[:, 2:3], math.log(128.0 ** -0.5), ['EPSC'])
    MEMSET(EPSC[:, 3:4], -math.log(128.0 ** -0.5), ['EPSC'])

    for j in range(NT):
        make_XT(j)
    for hb in range(2):
        for mt in range(2):
            b = bank()
            for c in range(4):
                cc = hb * 4 + c
                TR(ps[b][:, c * 128:(c + 1) * 128], MEMF[:, mt, cc * 128:(cc + 1) * 128], ['MEMF'], [('ps', b)])
            dst = MEMT[:, hb * 4:(hb + 1) * 4, mt * 128:(mt + 1) * 128]
            src = ps[b][:].rearrange("p (c t) -> p c t", c=4)
            P.add('act', lambda dst=dst, src=src: nc.scalar.copy(out=dst, in_=src), (), [('ps', b), 'MEMT'])

    marks = []
    stored = set()

    def mark(name):
        marks.append((name, sum(1 for o in P.ops if o[0] == 'pe')))

    def phase_reset():
        P.barrier()
        cur[0] = phase_base

    def cross_attn(l):
        mark('cross%d' % l)
        phase_reset()
        KMT = sb("KMT", [128, 8, MEM], BF16)
        VM = sb("VM", [128, 2, D], BF16)
        OT = sb("OT", [128, 8, S], BF16)
        part_base = cur[0]
        WS = [sb("WS%d" % i, [128, 8, 512], BF16) for i in range(2)]
        QT = [sb("QT%d" % i, [128, 2, S], BF16) for i in range(2)]
        WQ = [sb("WQ%d" % i, [128, 8, 256], BF16) for i in range(2)]
        PT = [sb("PT%d" % i, [128, 512], BF16) for i in range(4)]
        RS = [sb("RS%d" % i, [128, 512], F32) for i in range(2)]
        WO = sb("WO", [128, 8, D], BF16)
        epi_base = cur[0]
        st = {}
        pti = [0]

        def q_proj(h):
            wq = WQ[h % 2]
            wqk = ('WQ', h % 2)
            qt = QT[h % 2]
            for dc in range(2):
                for t4 in range(4):
                    b = bank()
                    for k in range(8):
                        MM(ps[b][:], wq[:, k, dc * 128:(dc + 1) * 128], XT[:, k, t4 * 512:(t4 + 1) * 512],
                           k == 0, k == 7, [wqk] + XTk(t4 * 4, t4 * 4 + 4), [('ps', b)])
                    ACT(qt[:, dc, t4 * 512:(t4 + 1) * 512], ps[b][:], AF.Identity, [], [('ps', b), ('QT', h % 2, t4)],
                        scale=1.0 / 16.0)

        LOADC(WQ[0][:], wslab(mem_wq[l][:, 0:256]), [('WQ', 0)])
        for s in range(2):
            LOADC(WS[s][:], wslab(mem_wkv[l][:, s * 512:(s + 1) * 512]), [('WS', s)])
        q_proj(0)
        for s in range(4):
            w = WS[s % 2]
            wk = ('WS', s % 2)
            if s >= 2:
                LOADC(w[:], wslab(mem_wkv[l][:, s * 512:(s + 1) * 512]), [wk])
            if s < 2:
                for m2 in range(4):
                    c = s * 4 + m2
                    b = bank()
                    for k in range(8):
                        MM(ps[b][:, 0:MEM], w[:, k, m2 * 128:(m2 + 1) * 128], MEMT[:, k, :], k == 0, k == 7,
                           [wk, 'MEMT'], [('ps', b)])
                    ACT(KMT[:, c, :], ps[b][:, 0:MEM], AF.Copy, [], [('ps', b), 'KMT'])
            else:
                for mt in range(2):
                    b = bank()
                    for k in range(8):
                        MM(ps[b][:], MEMT[:, k, mt * 128:(mt + 1) * 128], w[:, k, :], k == 0, k == 7,
                           [wk, 'MEMT'], [('ps', b)])
                    ACT(VM[:, mt, (s - 2) * 512:(s - 1) * 512], ps[b][:], AF.Copy, [], [('ps', b), 'VM'])

        def CQ(it):
            h, tg = it
            if tg != 0:
                return
            if h >= 1:
                q_proj(h)
            if h + 1 < 4:
                LOADC(WQ[(h + 1) % 2][:], wslab(mem_wq[l][:, (h + 1) * 256:(h + 2) * 256]), [('WQ', (h + 1) % 2)])
            if h == 2:
                for hf in range(2):
                    LOADC(WO[:, :, hf * 512:(hf + 1) * 512], wslab(mem_wo[l][:, hf * 512:(hf + 1) * 512]), ['WO'])

        def C1(it):
            h, tg = it
            qt = QT[h % 2]
            bs = []
            for mt in range(2):
                b = bank(hold=True)
                for dc in range(2):
                    MM(ps[b][:], KMT[:, 2 * h + dc, mt * 128:(mt + 1) * 128], qt[:, dc, tg * 512:(tg + 1) * 512],
                       dc == 0, dc == 1, ['KMT', ('QT', h % 2, tg)], [('ps', b)])
                bs.append(b)
            st[('c1', it)] = bs

        def C2(it):
            pts = []
            for mt in range(2):
                b = st[('c1', it)][mt]
                pi = pti[0] % 4
                pti[0] += 1
                ACT(PT[pi][:], ps[b][:], AF.Exp, [], [('ps', b), ('PT', pi)])
                free(b)
                pts.append(pi)
            st[('c2', it)] = pts

        def C3(it):
            h, tg = it
            pts = st[('c2', it)]
            b = bank(hold=True)
            for mt in range(2):
                MM(ps[b][:], ONESB[:], PT[pts[mt]][:], mt == 0, mt == 1, ['constb', ('PT', pts[mt])], [('ps', b)])
            bo = []
            for dc in range(2):
                b2 = bank(hold=True)
                for mt in range(2):
                    MM(ps[b2][:], VM[:, mt, h * 256 + dc * 128:h * 256 + (dc + 1) * 128], PT[pts[mt]][:],
                       mt == 0, mt == 1, ['VM', ('PT', pts[mt])], [('ps', b2)])
                bo.append(b2)
            st[('c3', it)] = (b, bo)

        def C4(it):
            h, tg = it
            b, bo = st[('c3', it)]
            ri = (h * 4 + tg) % 2
            RECIP(RS[ri][:], ps[b][:], [], [('ps', b), ('RS', ri)])
            for dc in range(2):
                TT(OT[:, 2 * h + dc, tg * 512:(tg + 1) * 512], ps[bo[dc]][:], RS[ri][:], ALU.mult,
                   [('RS', ri)], [('ps', bo[dc]), ('OT', tg)])
            free(b, *bo)
        pipeline([(h, tg) for h in range(4) for tg in range(4)], [CQ, C1, C2, C3, C4])
        P.barrier()
        cur[0] = part_base
        lb = ln_bufs(l, 1)
        obank = {}

        def O1(j):
            bs = []
            for hf in range(2):
                b = bank(hold=True)
                for k in range(8):
                    MM(ps[b][:], OT[:, k, j * 128:(j + 1) * 128], WO[:, k, hf * 512:(hf + 1) * 512],
                       k == 0, k == 7, ['WO'], [('ps', b)])
                bs.append(b)
            obank[j] = bs

        def O2(j):
            for hf in range(2):
                b = obank[j][hf]
                STT(X[:, j, hf * 512:(hf + 1) * 512], X[:, j, hf * 512:(hf + 1) * 512], ALPHA, ps[b][:],
                    ALU.mult, ALU.add, [], [('ps', b), ('X', j)])
                free(b)
        lst = ln_stages(lb)

        def O2La(j):
            O2(j)
            lst[0](j)
        pipeline(range(NT), [O1, O2La] + lst[1:])

    def ffn(l):
        mark('ffn%d' % l)
        phase_reset()
        HT = sb("HT", [128, 22, 1024], BF16)
        WG = [sb("WG%d" % i, [128, 8, 256], BF16) for i in range(2)]
        WU = [sb("WU%d" % i, [128, 8, 256], BF16) for i in range(2)]
        WD = [sb("WD%d" % i, [128, 22, 128], BF16) for i in range(2)]
        SG = [sb("SG%d" % i, [128, 512], F32) for i in range(2)]
        YT = [sb("YT%d" % i, [128, 512], F32) for i in range(2)]
        lb = ln_bufs(l, 2)
        lnst = ln_stages(lb)
        if l == DEPTH - 1:
            def Lstore(j):
                o = LOAD(y_d[j * 128:(j + 1) * 128, :], X[:, j, :], [], r=[('X', j), ('XA', j), ('XB', j)])
                P.final_wait.append(o)
                stored.add(j)
            lnst = lnst + [Lstore]
        cnt = [0]
        wi = 0
        di = [0]
        bg = None
        for half in range(2):
            t0 = half * 1024
            for s in range(11):
                wg, wu = WG[wi % 2], WU[wi % 2]
                kg, ku = ('WG', wi % 2), ('WU', wi % 2)
                wi += 1
                LOADC(wg[:], wslab(ffn_w_gu[l][:, s * 256:(s + 1) * 256]), [kg])
                LOADC(wu[:], wslab(ffn_w_gu[l][:, FFN + s * 256:FFN + (s + 1) * 256]), [ku])
                for m2 in range(2):
                    mc = s * 2 + m2
                    for tg2 in range(2):
                        tt0 = t0 + tg2 * 512
                        xk = XTk(tt0 // 128, tt0 // 128 + 4)
                        bgk = bank()
                        for k in range(8):
                            MM(ps[bgk][:], wg[:, k, m2 * 128:(m2 + 1) * 128], XT[:, k, tt0:tt0 + 512], k == 0, k == 7,
                               [kg] + xk, [('ps', bgk)])
                        bu = bank()
                        for k in range(8):
                            MM(ps[bu][:], wu[:, k, m2 * 128:(m2 + 1) * 128], XT[:, k, tt0:tt0 + 512], k == 0, k == 7,
                               [ku] + xk, [('ps', bu)])
                        si = cnt[0] % 2
                        cnt[0] += 1
                        ACT(SG[si][:], ps[bgk][:], AF.Silu, [], [('ps', bgk), ('SG', si)])
                        TT(HT[:, mc, tg2 * 512:(tg2 + 1) * 512], SG[si][:], ps[bu][:], ALU.mult,
                           [('SG', si)], [('ps', bu), ('HT', tg2)])
                        if bg is not None and (mc * 2 + tg2) % 3 == 0:
                            bg.tick()
            if bg is not None:
                bg.drain()
            st = {}

            def D1(it, half=half):
                m, tg2 = it
                if tg2 == 0:
                    wd = WD[di[0] % 2]
                    kd = ('WD', di[0] % 2)
                    st[('wd', m)] = (wd, kd)
                    di[0] += 1
                    LOADC(wd[:], ffn_w_down[l][:, m * 128:(m + 1) * 128].rearrange("(k p) n -> p k n", p=128), [kd])
                wd, kd = st[('wd', m)]
                b = bank(hold=True)
                for kc in range(22):
                    MM(ps[b][:], wd[:, kc, :], HT[:, kc, tg2 * 512:(tg2 + 1) * 512], kc == 0, kc == 21,
                       [kd, ('HT', tg2)], [('ps', b)])
                st[('d1', it)] = b

            def D2(it, half=half):
                b = st[('d1', it)]
                yi = cnt[0] % 2
                cnt[0] += 1
                ACT(YT[yi][:], ps[b][:], AF.Copy, [], [('ps', b), ('YT', yi)])
                free(b)
                st[('d2', it)] = yi

            def D3(it, half=half):
                yi = st[('d2', it)]
                b2 = bank(hold=True)
                for ts in range(4):
                    TR(ps[b2][:, ts * 128:(ts + 1) * 128], YT[yi][:, ts * 128:(ts + 1) * 128], [('YT', yi)], [('ps', b2)])
                st[('d3', it)] = b2

            def D4(it, half=half):
                m, tg2 = it
                b2 = st[('d3', it)]
                j0 = half * 8 + tg2 * 4
                xs = X[:, j0:j0 + 4, m * 128:(m + 1) * 128]
                STT(xs, xs, ALPHA, ps[b2][:].rearrange("p (a f) -> p a f", a=4), ALU.mult, ALU.add,
                    [], [('ps', b2)] + [('X', j) for j in range(j0, j0 + 4)])
                free(b2)
            pipeline([(m, tg2) for m in range(8) for tg2 in range(2)], [D1, D2, D3, D4])
            bg = Pipe(range(half * 8, half * 8 + 8), lnst)
        bg.drain()

    def odd_mixer(l):
        jj = l // 2
        lam_init = 0.8 - 0.6 * math.exp(-0.3 * l)
        mark('odd%d' % l)
        phase_reset()
        MT = sb("MT", [128, 8, S], BF16)
        WOUT = sb("WOUT", [128, 8, D], BF16)
        fin_base = cur[0]
        COS = sb("COS", [128, S], BF16)
        SIN = sb("SIN", [128, S], BF16)
        LAMP = sb("LAMP", [128, 4, 64], F32)
        LT = sb("LT", [128, 2, 64], F32)
        LS = sb("LS", [128, 8], F32)
        SUBC = sb("SUBC", [128, 2], F32)
        W5 = [{q: sb("W5_%d_%d" % (i, q), [128, 8, 128], BF16) for q in (0, 2, 4)} for i in range(2)]
        QTH = sb("QTH", [128, S], BF16)
        KTHC = [sb("KTHC%d" % i, [128, S], BF16) for i in range(2)]
        VH = sb("VH", [128, NT, 128], BF16)
        FW = [sb("FW%d" % i, [128, 512], F32) for i in range(6)]
        SQB = sb("SQB", [128, 512], BF16)
        QB = [sb("QB%d" % i, [128, 512], BF16) for i in range(2)]
        PERMB = sb("PERMB", [128, 128], BF16)
        LOADC(PERMB[:], permm_d, ['PERMB'])
        qbi = [0]
        PT = [sb("PTo%d" % i, [128, 512], BF16) for i in range(4)]
        for q in range(4):
            LOADC(COS[:, q * 512:(q + 1) * 512], cosT_d[:, q * 512:(q + 1) * 512], ['ROPE'])
            LOADC(SIN[:, q * 512:(q + 1) * 512], sinT_d[:, q * 512:(q + 1) * 512], ['ROPE'])
        LOAD(LAMP[:], lamp_d[jj].rearrange("p (a b) -> p a b", a=4), ['LAMP'])
        LOAD(SUBC[:, 0:1], subc_d[jj], ['SUBC'])
        TT(LT[:, 0, :], LAMP[:, 0, :], LAMP[:, 1, :], ALU.mult, ['LAMP'], ['LT'])
        TT(LT[:, 1, :], LAMP[:, 2, :], LAMP[:, 3, :], ALU.mult, ['LAMP'], ['LT'])
        P.add('dve', lambda: nc.vector.tensor_reduce(out=LS[:, 0:2], in_=LT[:], axis=AX.X, op=ALU.add), ['LT'], ['LS'])
        ACT(LS[:, 2:4], LS[:, 0:2], AF.Exp, ['LS'], ['LS2'])
        TT(LS[:, 4:5], LS[:, 3:4], LS[:, 2:3], ALU.subtract, ['LS2'], ['LS3'])
        TS(LS[:, 5:6], LS[:, 4:5], -lam_init, None, ALU.add, None, ['LS3'], ['NLAM'])
        NLAM = LS[:, 5:6]
        for c_ in range(2):
            for q in range(4):
                MEMSET(KTHC[c_][:, q * 512:(q + 1) * 512], 0.0, [('KTH', q)], eng='pool')
        ri = [0]
        pti = [0]
        deferred = []

        def tick():
            for d_ in deferred:
                d_[0] -= 1
            while deferred and deferred[0][0] <= 0:
                deferred.pop(0)[1]()

        def fw(i):
            return FW[i], ('FW', i)

        def load_w5(h):
            ws = W5[h % 2]
            wk = [('W5', h % 2, q) for q in range(5)]
            LOADC(ws[0][:], wslab(odd_w_in[jj][:, h * 128:(h + 1) * 128]), [wk[0]])
            LOADC(ws[2][:], wslab(odd_w_in[jj][:, D + h * 128:D + (h + 1) * 128]), [wk[2]])
            LOADC(ws[4][:], wslab(odd_w_in[jj][:, 2 * D + h * 128:2 * D + (h + 1) * 128]), [wk[4]])

        load_w5(0)
        for h in range(8):
            ws = W5[h % 2]
            wk = [('W5', h % 2, q) for q in range(5)]
            pb = (0, 1, 2, 3)
            gst = {}

            def G1(g):
                tick()
                (dst, dkey, wa), tg = g
                sl = slice(tg * 512, (tg + 1) * 512)
                ba = bank(hold=True)
                for k in range(8):
                    MM(ps[ba][:], ws[wa][:, k, :], XT[:, k, sl], k == 0, k == 7, [wk[wa]] + XTk(tg * 4, tg * 4 + 4), [('ps', ba)])
                gst[('a', g)] = ba

            def G2(g):
                ba = gst[('a', g)]
                qi = qbi[0] % 2
                qbi[0] += 1
                ACT(QB[qi][:], ps[ba][:], AF.Copy, [], [('ps', ba), ('QB', qi)])
                gst[('q', g)] = qi

            def G3(g):
                qi = gst[('q', g)]
                bb = bank(hold=True)
                MM(ps[bb][:], PERMB[:], QB[qi][:], True, True, ['PERMB', ('QB', qi)], [('ps', bb)])
                gst[('b', g)] = bb

            def G4(g):
                (dst, dkey, wa), tg = g
                sl = slice(tg * 512, (tg + 1) * 512)
                ba, bb = gst[('a', g)], gst[('b', g)]
                r = ri[0] % 2
                ri[0] += 1
                gst[('r', g)] = r
                (r1, k1), (r2, k2) = fw(4 + r), fw(2 + r)
                TT(r1[:], ps[ba][:], COS[:, sl], ALU.mult, ['ROPE'], [('ps', ba), k1])
                TT(r2[:], ps[bb][:], SIN[:, sl], ALU.mult, ['ROPE'], [('ps', bb), k2])
                free(ba, bb)

            def G5(g):
                (dst, dkey, wa), tg = g
                sl = slice(tg * 512, (tg + 1) * 512)
                r = gst[('r', g)]
                (r1, k1), (r2, k2) = fw(4 + r), fw(2 + r)
                if dst == 'K':
                    for c_ in range(2):
                        pr = slice(c_ * 64, (c_ + 1) * 64)
                        TT(KTHC[c_][pr, sl], r1[pr, :], r2[pr, :], ALU.add, [k1, k2], [(dkey, tg)], eng='pool')
                else:
                    TT(QTH[:, sl], r1[:], r2[:], ALU.add, [k1, k2], [(dkey, tg)], eng='pool')
            pipeline([(qk, tg) for qk in (('K', 'KTH', 2), ('Q', 'QTH', 0)) for tg in range(4)], [G1, G2, G3, G4, G5])
            for j4 in range(4):
                tick()
                b = bank(pb)
                for ts in range(4):
                    j = j4 * 4 + ts
                    for k in range(8):
                        MM(ps[b][:, ts * 128:(ts + 1) * 128], XT[:, k, j * 128:(j + 1) * 128], ws[4][:, k, :],
                           k == 0, k == 7, [wk[4], ('XT', j)], [('ps', b)])
                ACT(VH[:, j4 * 4:(j4 + 1) * 4, :], ps[b][:].rearrange("p (a f) -> p a f", a=4), AF.Copy,
                    [], [('ps', b), 'VH'])
            if h + 1 < 8:
                load_w5(h + 1)
            if h == 6:
                for hf in range(2):
                    LOADC(WOUT[:, :, hf * 512:(hf + 1) * 512], wslab(odd_w_out[jj][:, hf * 512:(hf + 1) * 512]), ['WOUT'])
            st = {}

            def A1(it):
                tick()
                tg, kt, comp = it
                b = bank((0, 1, 2, 3), hold=True)
                MM(ps[b][:], KTHC[comp][:, kt * 128:(kt + 1) * 128], QTH[:, tg * 512:(tg + 1) * 512], True, True,
                   [('KTH', kt // 4), ('QTH', tg)], [('ps', b)])
                st[('a1', it)] = b

            def A2(it):
                b = st[('a1', it)]
                pi = pti[0] % 4
                pti[0] += 1
                ACT(PT[pi][:], ps[b][:], AF.Exp, [], [('ps', b), ('PT', pi)], scale=0.125)
                free(b)
                st[('a2', it)] = pi

            def A3(it, h=h):
                tg, kt, comp = it
                pi = st[('a2', it)]
                bo, bd = 4 + 2 * comp, 5 + 2 * comp
                MM(ps[bo][:], VH[:, kt, :], PT[pi][:], kt == 0, kt == NT - 1, [('PT', pi), 'VH'], [('ps', bo)])
                MM(ps[bd][:], ONESB[:], PT[pi][:], kt == 0, kt == NT - 1, [('PT', pi), 'constb'], [('ps', bd)])
                if not (kt == NT - 1 and comp == 1):
                    return
                (O0, kO0), (D0, kD0), (O1, kO1), (D1, kD1) = fw(0), fw(1), fw(2), fw(3)
                ACT(O0[:], ps[4][:], AF.Copy, [], [('ps', 4), kO0])
                P.add('dve', lambda: nc.vector.tensor_copy(out=D0[:], in_=ps[5][:]), (), [('ps', 5), kD0])
                ACT(O1[:], ps[6][:], AF.Copy, [], [('ps', 6), kO1])
                P.add('dve', lambda: nc.vector.tensor_copy(out=D1[:], in_=ps[7][:]), (), [('ps', 7), kD1])
                RECIP(D0[:], D0[:], [], [kD0])
                TT(O0[:], O0[:], D0[:], ALU.mult, [kD0], [kO0], eng='pool')
                RECIP(D1[:], D1[:], [], [kD1])
                TT(O1[:], O1[:], D1[:], ALU.mult, [kD1], [kO1], eng='pool')
                STT(O0[:], O1[:], NLAM, O0[:], ALU.mult, ALU.add, ['NLAM', kO1], [kO0])

                def tail1():
                    ACT(SQB[:], O0[:], AF.Square, [kO0], ['SQB'])

                def tail2(h=h, tg=tg):
                    bss = bank((0, 1, 2, 3), hold=True)
                    MM(ps[bss][:], ONESB[:], SQB[:], True, True, ['SQB', 'constb'], [('ps', bss)])
                    ACT(D0[:], ps[bss][:], AF.Ln, [], [('ps', bss), kD0], bias=EPSC[:, 0:1], scale=1.0 / 128.0)
                    free(bss)
                    ACT(D0[:], D0[:], AF.Exp, [], [kD0], scale=-0.5)
                    TT(O0[:], O0[:], D0[:], ALU.mult, [kD0], [kO0], eng='pool')
                    TS(MT[:, h, tg * 512:(tg + 1) * 512], O0[:], SUBC[:, 0:1], 1.0 - lam_init, ALU.mult, ALU.mult,
                       [kO0, 'SUBC'], [('MT', tg)])
                deferred.append([22, tail1])
                deferred.append([26, tail2])
            pipeline([(tg, kt, comp) for tg in range(4) for kt in range(NT) for comp in range(2)], [A1, A2, A3], gap=2)
        while deferred:
            deferred.pop(0)[1]()
        mark('oddfin%d' % l)
        P.barrier()
        cur[0] = fin_base
        lb = ln_bufs(l, 0)
        epilogue([], WOUT, 'WOUT', lb, SRC=MT)

    def even_mixer(l):
        jj = l // 2
        mark('even%d' % l)
        phase_reset()
        BO = sb("BO", [128, NT, 512], BF16)
        H = sb("H", [128, NT, 512], F32)
        fin_base = cur[0]
        CW = sb("CW", [128, 8, 5], F32)
        GA = sb("GA", [128, NT, 16], F32)
        GBR = sb("GBR", [128, NT, 16], F32)
        SP = sb("SP", [128, NT, 8], F32)
        UE = sb("UE", [128, NT, 8], F32)
        RE = sb("RE", [128, NT, 8], F32)
        GD = sb("GD", [128, NT, 8], F32)
        WGT = sb("WGT", [128, 8, 16], BF16)
        W3 = [sb("W3_%d" % q, [128, 8, 128], BF16) for q in range(3)]
        head_base = cur[0]
        LOAD(CW[:], conv_d[jj], ['CW'])
        LOAD(GBR[:], gateb_d[jj].rearrange("p (a b) -> p a b", a=NT), ['GBR'])
        for q in range(4):
            MEMSET(H[:, q * 4:(q + 1) * 4, :], 0.0, [('H', j) for j in range(q * 4, q * 4 + 4)], eng='pool')
        BQT = sb("BQT", [128, S], BF16)
        BKTC = [sb("BKTC%d" % i, [128, S], BF16) for i in range(2)]
        BV = sb("BV", [128, NT, 2, 65], BF16)
        BMs = [sb("BM%d" % i, [128, NVAR, 2, 128], BF16) for i in range(2)]
        PN = [sb("PN%d" % i, [128, 512], BF16) for i in range(8)]
        RDN = sb("RDN", [128, 4], F32)
        MEMSET(BV[:, :, :, 64:65], 1.0, ['BV'])
        for c_ in range(2):
            for q in range(4):
                MEMSET(BKTC[c_][:, q * 512:(q + 1) * 512], 0.0, [('BKT', q)], eng='pool')
        pni = [0]
        rdi = [0]
        for cb in range(4):
            wk = [('W3', q) for q in range(3)]
            for q, c0 in enumerate((2064, 2576, 3088)):
                LOADC(W3[q][:], wslab(even_w_in[jj][:, c0 + cb * 128:c0 + (cb + 1) * 128]), [wk[q]])
            for v0 in range(0, NVAR, 4):
                v1 = min(NVAR, v0 + 4)
                LOADC(BMs[cb % 2][:, v0:v1, :, :], nabm_d[jj, cb][:, v0:v1, :, :], [('BM', cb % 2)])
            for (dst, dkey, q, sc) in ((BQT, 'BQT', 0, 0.125), (None, 'BKT', 1, 1.0)):
                for tg in range(4):
                    sl = slice(tg * 512, (tg + 1) * 512)
                    b = bank()
                    for k in range(8):
                        MM(ps[b][:], W3[q][:, k, :], XT[:, k, sl], k == 0, k == 7, [wk[q]] + XTk(tg * 4, tg * 4 + 4), [('ps', b)])
                    if dst is None:
                        for c_ in range(2):
                            pr = slice(c_ * 64, (c_ + 1) * 64)
                            ACT(BKTC[c_][pr, sl], ps[b][pr, :], AF.Copy, [], [('ps', b), (dkey, tg)])
                    else:
                        ACT(dst[:, sl], ps[b][:], AF.Identity, [], [('ps', b), (dkey, tg)], scale=sc)
            for j4 in range(4):
                b = bank()
                for ts in range(4):
                    j = j4 * 4 + ts
                    for k in range(8):
                        MM(ps[b][:, ts * 128:(ts + 1) * 128], XT[:, k, j * 128:(j + 1) * 128], W3[2][:, k, :],
                           k == 0, k == 7, [wk[2], ('XT', j)], [('ps', b)])
                for hh in range(2):
                    ACT(BV[:, j4 * 4:(j4 + 1) * 4, hh, 0:64],
                        ps[b][:].rearrange("p (a f) -> p a f", a=4)[:, :, hh * 64:(hh + 1) * 64], AF.Copy,
                        [], [('ps', b), 'BV'])
            st = {}
            items = [(jt, hh) for jt in range(NT) for hh in range(2)]

            def N1(it, cb=cb):
                jt, hh = it
                lst = _NA_PLAN[jt]
                bs = []
                for g0 in range(0, len(lst), 4):
                    b = bank((0, 1, 2, 3, 4, 5), hold=True)
                    for i, (kt, var) in enumerate(lst[g0:g0 + 4]):
                        reg = ps[b][:, i * 128:(i + 1) * 128]
                        MM(reg, BKTC[hh][:, kt * 128:(kt + 1) * 128], BQT[:, jt * 128:(jt + 1) * 128], True, False,
                           [('BKT', kt // 4), ('BQT', jt // 4)], [('ps', b)])
                        MM(reg, IDB[:], BMs[cb % 2][:, var, hh, :], False, True, ['constb', ('BM', cb % 2)], [('ps', b)])
                    bs.append((b, len(lst[g0:g0 + 4])))
                st[('n1', it)] = bs

            def N2(it):
                out = []
                for (b, n) in st[('n1', it)]:
                    pi = pni[0] % 8
                    pni[0] += 1
                    ACT(PN[pi][:, 0:n * 128], ps[b][:, 0:n * 128], AF.Exp, [], [('ps', b), ('PN', pi)])
                    free(b)
                    out.append((pi, n))
                st[('n2', it)] = out

            def N3(it, cb=cb):
                jt, hh = it
                h = 2 * cb + hh
                lst = _NA_PLAN[jt]
                bo = bank((6, 7), hold=True)
                idx = 0
                for (pi, n) in st[('n2', it)]:
                    for i in range(n):
                        kt = lst[idx][0]
                        MM(ps[bo][:, 0:65], PN[pi][:, i * 128:(i + 1) * 128], BV[:, kt, hh, :], idx == 0, idx == len(lst) - 1,
                           [('PN', pi), 'BV'], [('ps', bo)])
                        idx += 1
                r = rdi[0] % 4
                rdi[0] += 1
                RECIP(RDN[:, r:r + 1], ps[bo][:, 64:65], [], [('ps', bo), ('RDN', r)])
                TS(BO[:, jt, h * 64:(h + 1) * 64], ps[bo][:, 0:64], RDN[:, r:r + 1], None, ALU.mult, None,
                   [('RDN', r)], [('ps', bo), ('BO', jt)])
                free(bo)
            pipeline(items, [N1, N2, N3], gap=2)
        mark('mlstm%d' % l)
        P.barrier()
        cur[0] = head_base
        RAWP = sb("RAWP", [128, S + 4], BF16)
        DG = [sb("DG%d" % i, [128, 5, 128], BF16) for i in range(2)]
        CFG = sb("CFG", [128, 2, 129], F32)
        QCs = [sb("QC%d" % i, [128, S], BF16) for i in range(2)]
        KCs = [sb("KC%d" % i, [128, S], BF16) for i in range(2)]
        KTOKs = [sb("KTOK%d" % i, [128, NT, 128], BF16) for i in range(2)]
        VHs = [sb("VHe%d" % i, [128, NT, 129], BF16) for i in range(2)]
        CF = sb("CF", [128, 2, 129], F32)
        CB = sb("CB", [128, 2, 129], BF16)
        SMT = [sb("SMT%d" % i, [128, 128], BF16) for i in range(8)]
        VP = [sb("VP%d" % i, [128, 129], BF16) for i in range(8)]
        SM = sb("SM", [128, 8, 4], F32)
        GTMP = sb("GTMP", [128, NT, 8], F32)
        IRE = sb("IRE", [128, NT, 8], F32)
        LOADC(WGT[:], wslab(even_w_in[jj][:, 2048:2064]), ['WGT'])
        bg = bank()
        for jt in range(NT):
            for k in range(8):
                MM(ps[bg][:, jt * 16:(jt + 1) * 16], XT[:, k, jt * 128:(jt + 1) * 128], WGT[:, k, :], k == 0, k == 7,
                   ['WGT', ('XT', jt)], [('ps', bg)])
        TT(GA[:], ps[bg][:, 0:256].rearrange("p (a b) -> p a b", a=NT), GBR[:], ALU.add, ['GBR'], [('ps', bg), 'GA'])
        ACT(GTMP[:], GA[:, :, 8:16], AF.Exp, ['GA'], ['GTMP'], scale=-1.0)
        ACT(SP[:], GTMP[:], AF.Ln, ['GTMP'], ['SP'], bias=EPSC[:, 1:2], scale=1.0)
        bc = bank()
        bt = bank()
        for jt in range(NT):
            MM(ps[bc][:, jt * 8:jt * 8 + 4], TRIF[:], SP[:, jt, 0:4], True, True, ['const', 'SP'], [('ps', bc)])
            MM(ps[bc][:, jt * 8 + 4:jt * 8 + 8], TRIB[:], SP[:, jt, 4:8], True, True, ['const', 'SP'], [('ps', bc)])
            MM(ps[bt][:, jt * 8:jt * 8 + 8], ONES[:], SP[:, jt, :], True, True, ['const', 'SP'], [('ps', bt)])
        csv = ps[bc][:, 0:128].rearrange("p (a b) -> p a b", a=NT)
        TT(GTMP[:], GA[:, :, 0:8], csv, ALU.add, ['GA'], [('ps', bc), 'GTMP'])
        ACT(UE[:], GTMP[:], AF.Exp, ['GTMP'], ['UE'])
        ACT(RE[:], csv, AF.Exp, [], [('ps', bc), 'RE'], bias=EPSC[:, 2:3], scale=-1.0)
        ACT(IRE[:], csv, AF.Exp, [], [('ps', bc), 'RE'], bias=EPSC[:, 3:4], scale=1.0)
        ACT(GD[:], ps[bt][:, 0:128].rearrange("p (a b) -> p a b", a=NT), AF.Exp, [], [('ps', bt), 'GD'], scale=-1.0)
        for i_ in range(2):
            MEMSET(VHs[i_][:, :, 128:129], 1.0, [('VH', i_)])
        MEMSET(RAWP[:, 0:2], 0.0, [('RAW', -1)])
        MEMSET(RAWP[:, S + 2:S + 4], 0.0, [('RAW', 4)])
        rot = [0]
        dgi = [0]
        def proj_chunks(h):
            hp = h % 2
            QC, KC, KTOK, VH = QCs[hp], KCs[hp], KTOKs[hp], VHs[hp]
            kQ, kK, kT, kV = ('QC', hp), ('KC', hp), ('KTOK', hp), ('VH', hp)
            wk = [('W3', q) for q in range(3)]
            ch = []

            def c_load():
                for q, c0 in enumerate((0, 512, 1024)):
                    LOADC(W3[q][:], wslab(even_w_in[jj][:, c0 + h * 128:c0 + (h + 1) * 128]), [wk[q]])
            ch.append(c_load)
            for (dst, dkey, q, cch) in ((QC, kQ, 0, h), (KC, kK, 1, 4 + h)):
                cell = {}

                def c_dg(cch=cch, cell=cell):
                    cell['dg'] = DG[dgi[0] % 2]
                    cell['dgk'] = ('DG', dgi[0] % 2)
                    dgi[0] += 1
                    for kk in range(5):
                        TS(cell['dg'][:, kk, :], IDB[:], CW[:, cch, kk:kk + 1], None, ALU.mult, None, ['constb', 'CW'], [cell['dgk']])
                ch.append(c_dg)
                for tg in range(4):
                    def c_proj(tg=tg, q=q):
                        sl = slice(tg * 512, (tg + 1) * 512)
                        b = bank()
                        for k in range(8):
                            MM(ps[b][:], W3[q][:, k, :], XT[:, k, sl], k == 0, k == 7, [wk[q]] + XTk(tg * 4, tg * 4 + 4), [('ps', b)])
                        ACT(RAWP[:, 2 + tg * 512:2 + (tg + 1) * 512], ps[b][:], AF.Copy, [], [('ps', b), ('RAW', tg)])
                    ch.append(c_proj)
                for tg in range(4):
                    def c_conv(tg=tg, cell=cell, dst=dst, dkey=dkey):
                        b = bank()
                        for kk in range(5):
                            MM(ps[b][:], cell['dg'][:, kk, :], RAWP[:, tg * 512 + kk:tg * 512 + kk + 512], kk == 0, kk == 4,
                               [cell['dgk'], ('RAW', tg - 1), ('RAW', tg), ('RAW', tg + 1)], [('ps', b)])
                        ACT(dst[:, tg * 512:(tg + 1) * 512], ps[b][:], AF.Silu, [], [('ps', b), dkey])
                    ch.append(c_conv)
            for j4 in range(4):
                def c_v(j4=j4):
                    b = bank()
                    for ts in range(4):
                        j = j4 * 4 + ts
                        for k in range(8):
                            MM(ps[b][:, ts * 128:(ts + 1) * 128], XT[:, k, j * 128:(j + 1) * 128], W3[2][:, k, :],
                               k == 0, k == 7, [wk[2], ('XT', j)], [('ps', b)])
                    ACT(VH[:, j4 * 4:(j4 + 1) * 4, 0:128], ps[b][:].rearrange("p (a f) -> p a f", a=4), AF.Copy,
                        [], [('ps', b), kV])
                ch.append(c_v)

                def c_kt(j4=j4):
                    b = bank()
                    for ts in range(4):
                        j = j4 * 4 + ts
                        MM(ps[b][:, ts * 128:(ts + 1) * 128], KC[:, j * 128:(j + 1) * 128], IDB[:], True, True,
                           [kK, 'constb'], [('ps', b)])
                    ACT(KTOK[:, j4 * 4:(j4 + 1) * 4, :], ps[b][:].rearrange("p (a f) -> p a f", a=4), AF.Copy, [], [('ps', b), kT])
                ch.append(c_kt)
            return ch

        for c_ in proj_chunks(0):
            c_()
        for h in range(4):
            hp = h % 2
            QC, KC, KTOK, VH = QCs[hp], KCs[hp], KTOKs[hp], VHs[hp]
            kQ, kK, kT, kV = ('QC', hp), ('KC', hp), ('KTOK', hp), ('VH', hp)
            nxt = proj_chunks(h + 1) if h + 1 < 4 else []
            MEMSET(CF[:], 0.0, [('CF', 0), ('CF', 1)])
            MEMSET(CB[:], 0.0, [('CB', 0), ('CB', 1)])
            MEMSET(CFG[:], 0.0, [('CFG', 0), ('CFG', 1)])
            st = {}

            def geo(it, h=h):
                step, d = it
                c = step if d == 0 else NT - 1 - step
                return c, d * 4 + h, slice(c * 128, (c + 1) * 128)

            def MA(it):
                if nxt:
                    nxt.pop(0)()
                c, col, tsl = geo(it)
                r = rot[0] % 8
                rot[0] += 1
                st[('r', it)] = r
                b1 = bank(hold=True)
                MM(ps[b1][:, 0:128], KC[:, tsl], QC[:, tsl], True, True, [kK, kQ], [('ps', b1)])
                st[('b1', it)] = b1
                ACT(VP[r][:], VH[:, c, :], AF.Identity, [kV, 'UE'], [('VP', r)], scale=UE[:, c, col:col + 1])

            def MB(it):
                d = it[1]
                r = st[('r', it)]
                b1 = st[('b1', it)]
                TT(SMT[r][:], ps[b1][:, 0:128], (TRIF if d == 0 else TRIB)[:], ALU.mult, ['const'],
                   [('ps', b1), ('SMT', r)])
                free(b1)

            def MC(it):
                c, col, tsl = geo(it)
                d = it[1]
                r = st[('r', it)]
                b3 = bank(hold=True)
                MM(ps[b3][:, 0:129], KTOK[:, c, :], VP[r][:], True, True, [kT, ('VP', r)], [('ps', b3)])
                b2 = bank(hold=True)
                MM(ps[b2][:, 0:129], SMT[r][:], VP[r][:], True, False, [('SMT', r), ('VP', r)], [('ps', b2)])
                MM(ps[b2][:, 0:129], QC[:, tsl], CB[:, d, :], False, True, [kQ, ('CB', d)], [('ps', b2)])
                st[('b2', it)] = (b2, b3)

            def MD1(it):
                c, col, tsl = geo(it)
                step, d = it
                b2, b3 = st[('b2', it)]
                g = GD[:, c, col:col + 1]
                STT(CB[:, d, :], ps[b3][:, 0:129], g, CFG[:, d, :], ALU.mult, ALU.add, ['GD', ('CFG', d)], [('ps', b3), ('CB', d)])
                STT(CF[:, d, :], ps[b3][:, 0:129], g, CFG[:, d, :], ALU.mult, ALU.add, ['GD', ('CFG', d)], [('ps', b3), ('CF', d)])
                free(b3)
                if step + 1 < NT:
                    cn, coln, _ = geo((step + 1, d))
                    TS(CFG[:, d, :], CF[:, d, :], GD[:, cn, coln:coln + 1], None, ALU.mult, None, [('CF', d), 'GD'], [('CFG', d)],
                       eng='pool')

            def MD2(it):
                c, col, tsl = geo(it)
                r = st[('r', it)]
                b2, b3 = st[('b2', it)]
                ACT(SM[:, r, 3:4], ps[b2][:, 128:129], AF.Abs, [], [('ps', b2), ('SM0', r)])

            def MD3(it, h=h):
                c, col, tsl = geo(it)
                r = st[('r', it)]
                b2, b3 = st[('b2', it)]
                TS(SM[:, r, 0:1], SM[:, r, 3:4], IRE[:, c, col:col + 1], None, ALU.max, None, [('SM0', r), 'RE'], [('SM', r)])
                RECIP(SM[:, r, 2:3], SM[:, r, 0:1], [('SM', r)], [('SM2', r)])
                hs = H[:, c, h * 128:(h + 1) * 128]
                STT(hs, ps[b2][:, 0:128], SM[:, r, 2:3], hs, ALU.mult, ALU.add, [('SM2', r)], [('ps', b2), ('H', c)])
                free(b2)
            pipeline([(step, d) for step in range(NT) for d in range(2)], [MA, MB, MC, MD1, MD2, MD3])
            while nxt:
                nxt.pop(0)()
        mark('evfin%d' % l)
        P.barrier()
        cur[0] = fin_base
        WOG = sb("WOG", [128, 8, 512], BF16)
        WOUT = sb("WOUTe", [128, 8, D], BF16)
        SGE = [sb("SGE%d" % i, [128, 512], F32) for i in range(2)]
        AO = [sb("AO%d" % i, [128, D], F32) for i in range(2)]
        lb = ln_bufs(l, 0)
        LOADC(WOG[:], wslab(even_w_in[jj][:, 1536:2048]), ['WOG'])
        for hf in range(2):
            LOADC(WOUT[:, :, hf * 512:(hf + 1) * 512], wslab(even_w_out[jj][:, hf * 512:(hf + 1) * 512]), ['WOUT'])
        st = {}

        def E1(j):
            b = bank(hold=True)
            for k in range(8):
                MM(ps[b][:], XT[:, k, j * 128:(j + 1) * 128], WOG[:, k, :], k == 0, k == 7, [('XT', j), 'WOG'], [('ps', b)])
            st[j] = b

        def E2(j):
            b = st[j]
            i = j % 2
            ACT(SGE[i][:], ps[b][:], AF.Sigmoid, [], [('ps', b), ('SGE', i)])
            free(b)
            TT(AO[i][:, 0:512], SGE[i][:], H[:, j, :], ALU.mult, [('SGE', i), ('H', j)], [('AO', i)])
            ACT(AO[i][:, 512:1024], BO[:, j, :], AF.Copy, [('BO', j)], [('AO', i)])

        def E3(j):
            i = j % 2
            make_XT(j, AO[i], ('AO', i))
        epilogue([E1, E2, E3], WOUT, 'WOUT', lb)

    n = 0
    for l in range(DEPTH):
        for sub in range(3):
            if n >= nsub:
                break
            if sub == 0:
                (even_mixer if l % 2 == 0 else odd_mixer)(l)
            elif sub == 1:
                cross_attn(l)
            else:
                ffn(l)
            n += 1
    P.barrier()
    yv = y_d.rearrange("(j p) d -> p j d", p=128)
    for q in range(4):
        if all(j in stored for j in range(q * 4, q * 4 + 4)):
            continue
        o = LOAD(yv[:, q * 4:(q + 1) * 4, :], X[:, q * 4:(q + 1) * 4, :], [], r=[('X', j) for j in range(q * 4, q * 4 + 4)])
        P.final_wait.append(o)
    mark('end')
    stats = P.emit()
    stats['marks'] = marks
    return nc, stats, list(declared.keys())


_CACHE = {}


def run(inputs, nsub=3 * DEPTH):
    inp = {k: np.ascontiguousarray(np.asarray(v, dtype=np.float32)) for k, v in inputs.items()}
    c = host_prep(inp)
    if nsub not in _CACHE:
        _CACHE[nsub] = build(nsub)
    nc, stats, names = _CACHE[nsub]
    allsrc = dict(inp)
    allsrc.update(c)

    def fetch(name):
        if name in allsrc and name not in ('x', 'mem'):
            return allsrc[name]
        base, idx = name, []
        while base not in allsrc:
            base, _, t = base.rpartition('_')
            idx.insert(0, int(t))
        return np.ascontiguousarray(allsrc[base][tuple(idx)])

    shared = {n: fetch(n) for n in names if n not in ('x', 'mem')}
    in_maps = []
    for core in range(8):
        m = dict(shared)
        m['x'] = np.ascontiguousarray(inp['x'][core])
        m['mem'] = np.ascontiguousarray(inp['mem'][core])
        in_maps.append(m)
    res = run_bass_kernel_spmd(nc, in_maps, core_ids=list(range(8)))
    return np.stack([np.asarray(res.results[i]['y'], dtype=np.float32) for i in range(8)], axis=0)


def kernel(**inputs):
    return run(inputs)
```
